# Optimizing a Trainium2 kernel written in Bass

```python
import math
import jax, jax.numpy as jnp
from jax import lax
import numpy as np

D_MODEL = 1024
BATCH = 8
SEQ = 8192
DEPTH = 4

HEAD_DIM = 64
N_HEADS = D_MODEL // HEAD_DIM
HEADS_A = 3 * N_HEADS // 8
HEADS_B = N_HEADS // 4
KV_HEADS_B = HEADS_B // 2
HEADS_C = N_HEADS - HEADS_A - HEADS_B
DILATED_PAIRS = ((128, 1), (512, 4), (2048, 16))
WINDOW_B = 128
MLSTM_CHUNK = 64
CONV_WIDTH = 5
FORGET_BIAS_LO = 3.0
FORGET_BIAS_HI = 6.0
N_BUCKETS = 32
REL_MAX_DIST = 1024
D_FF = -(-8 * D_MODEL // (3 * 256)) * 256
PLE_DIM = 256
EPS = 1e-6
NEG = -1e30
COL_SIZES = (HEADS_A * HEAD_DIM,) * 3 + (HEADS_B * HEAD_DIM, KV_HEADS_B * HEAD_DIM, KV_HEADS_B * HEAD_DIM) + (HEADS_C * HEAD_DIM,) * 4 + (4 * HEADS_C,)
D_IN = sum(COL_SIZES)

kernel_name = "hybrid_dilated_swa_mlstm_encoder"


def rmsnorm(x, g):
    xf = x.astype(jnp.float32)
    y = xf * lax.rsqrt(jnp.mean(xf * xf, axis=-1, keepdims=True) + EPS)
    return (y * g.astype(jnp.float32)).astype(x.dtype)


def t5_bucket(rel):
    half_b = N_BUCKETS // 2
    max_exact = half_b // 2
    n = jnp.abs(rel)
    nf = jnp.maximum(n, 1).astype(jnp.float32)
    large = max_exact + (jnp.log(nf / max_exact) / math.log(REL_MAX_DIST / max_exact) * (half_b - max_exact)).astype(jnp.int32)
    large = jnp.minimum(large, half_b - 1)
    return jnp.where(rel > 0, half_b, 0) + jnp.where(n < max_exact, n, large)


def banded_attention(q, k, v, bias_table, half, blk, dil, sink=None):
    N, L, H, dh = q.shape
    G = k.shape[2]
    R = H // G
    nb = -(-L // blk)
    Lp = nb * blk
    pad = Lp - L
    qp = jnp.pad(q, ((0, 0), (0, pad), (0, 0), (0, 0))).reshape(N, nb, blk, G, R, dh)

    def kblocks(t):
        tp = jnp.pad(t, ((0, 0), (blk, pad + blk), (0, 0), (0, 0))).reshape(N, nb + 2, blk, G, dh)
        return jnp.concatenate([tp[:, :-2], tp[:, 1:-1], tp[:, 2:]], axis=2)

    kb, vb = kblocks(k), kblocks(v)
    logits = jnp.einsum('nbqgrd,nbkgd->nbgrqk', qp, kb).astype(jnp.float32) * (dh ** -0.5)
    iq = jnp.arange(blk, dtype=jnp.int32)
    ik = jnp.arange(3 * blk, dtype=jnp.int32)
    rel = ik[None, :] - blk - iq[:, None]
    kpos = jnp.arange(nb, dtype=jnp.int32)[:, None] * blk - blk + ik[None, :]
    mask = (jnp.abs(rel) <= half)[None] & ((kpos >= 0) & (kpos < L))[:, None, :]
    bias = jnp.transpose(bias_table[t5_bucket(rel * dil)], (2, 0, 1)).reshape(G, R, blk, 3 * blk)
    logits = jnp.where(mask[None, :, None, None], logits + bias.astype(jnp.float32), NEG)
    m = jnp.max(logits, axis=-1)
    if sink is not None:
        s = sink.astype(jnp.float32).reshape(G, R)[None, None, :, :, None]
        m = jnp.maximum(m, s)
    pexp = jnp.exp(logits - m[..., None])
    denom = jnp.sum(pexp, axis=-1)
    if sink is not None:
        denom = denom + jnp.exp(s - m)
    out = jnp.einsum('nbgrqk,nbkgd->nbqgrd', pexp, vb.astype(jnp.float32))
    out = out / jnp.moveaxis(denom, -1, 2)[..., None]
    lse = jnp.moveaxis(m + jnp.log(denom), -1, 2)
    return out.reshape(N, Lp, H, dh)[:, :L], lse.reshape(N, Lp, H)[:, :L]


def dilated_attention(q, k, v, bias_table):
    B, S, H, dh = q.shape
    outs, lses = [], []
    for (w, d) in DILATED_PAIRS:
        half = w // (2 * d)

        def to_sub(t):
            return t.reshape(B, S // d, d, H, dh).transpose(0, 2, 1, 3, 4).reshape(B * d, S // d, H, dh)

        o, l = banded_attention(to_sub(q), to_sub(k), to_sub(v), bias_table, half, half, d)
        outs.append(o.reshape(B, d, S // d, H, dh).transpose(0, 2, 1, 3, 4).reshape(B, S, H, dh))
        lses.append(l.reshape(B, d, S // d, H).transpose(0, 2, 1, 3).reshape(B, S, H))
    wts = jax.nn.softmax(jnp.stack(lses, 0), axis=0)
    return jnp.einsum('gbsh,gbshd->bshd', wts, jnp.stack(outs, 0))


def mlstm_direction(q, k, v, ig, fg):
    N, H, S, dh = q.shape
    L = MLSTM_CHUNK
    nc = S // L
    q = q.reshape(N, H, nc, L, dh)
    k = k.reshape(N, H, nc, L, dh)
    v = v.reshape(N, H, nc, L, dh)
    ig = ig.reshape(N, H, nc, L)
    b = jnp.cumsum(jax.nn.log_sigmoid(fg).reshape(N, H, nc, L), axis=-1)
    b_last = b[..., -1]
    w = b_last[..., None] - b + ig
    m_loc = jnp.max(w, axis=-1)
    e = jnp.exp(w - m_loc[..., None])
    C_loc = jnp.einsum('nhcs,nhcsd,nhcse->nhcde', e, k, v)
    n_loc = jnp.einsum('nhcs,nhcsd->nhcd', e, k)

    def step(carry, xs):
        C, n, m = carry
        bl, Cl, nl, ml = xs
        m_new = jnp.maximum(bl + m, ml)
        a = jnp.exp(bl + m - m_new)
        c = jnp.exp(ml - m_new)
        C_new = a[..., None, None] * C + c[..., None, None] * Cl
        n_new = a[..., None] * n + c[..., None] * nl
        return (C_new, n_new, m_new), (C, n, m)

    init = (jnp.zeros((N, H, dh, dh), jnp.float32), jnp.zeros((N, H, dh), jnp.float32), jnp.zeros((N, H), jnp.float32))
    xs = tuple(jnp.moveaxis(t, 2, 0) for t in (b_last, C_loc, n_loc, m_loc))
    _, (C_prev, n_prev, m_prev) = lax.scan(step, init, xs)
    C_prev = jnp.moveaxis(C_prev, 0, 2)
    n_prev = jnp.moveaxis(n_prev, 0, 2)
    m_prev = jnp.moveaxis(m_prev, 0, 2)
    lower = jnp.tril(jnp.ones((L, L), dtype=bool))
    D = jnp.where(lower, b[..., :, None] - b[..., None, :] + ig[..., None, :], NEG)
    m_inter = b + m_prev[..., None]
    m_t = jnp.maximum(jnp.max(D, axis=-1), m_inter)
    P = jnp.exp(D - m_t[..., None]) * jnp.einsum('nhctd,nhcsd->nhcts', q, k)
    a = jnp.exp(m_inter - m_t)
    num = jnp.einsum('nhcts,nhcse->nhcte', P, v) + a[..., None] * jnp.einsum('nhctd,nhcde->nhcte', q, C_prev)
    den = jnp.sum(P, axis=-1) + a * jnp.einsum('nhctd,nhcd->nhct', q, n_prev)
    h = num / jnp.maximum(jnp.abs(den), jnp.exp(-m_t))[..., None]
    return h.reshape(N, H, S, dh)


def centred_dwconv(t, w):
    K, C = w.shape
    return lax.conv_general_dilated(t, w[:, None, :], window_strides=(1,), padding=[(K // 2, K // 2)], dimension_numbers=('NWC', 'WIO', 'NWC'), feature_group_count=C)


def mlstm_mixer(qc, kc, vc, oc, gc, conv_w, gate_b, norm_g):
    B, S, _ = qc.shape
    dt = qc.dtype
    qk = jax.nn.silu(centred_dwconv(jnp.concatenate([qc, kc], axis=-1), conv_w))
    q, k = jnp.split(qk, 2, axis=-1)

    def to_bhsd(t):
        return t.reshape(B, S, HEADS_C, HEAD_DIM).transpose(0, 2, 1, 3).astype(jnp.float32)

    q, k, v = to_bhsd(q), to_bhsd(k) * (HEAD_DIM ** -0.5), to_bhsd(vc)
    g = (gc + gate_b).astype(jnp.float32).reshape(B, S, 4, HEADS_C).transpose(2, 0, 3, 1)
    ig_f, fg_f, ig_b, fg_b = g[0], g[1], g[2], g[3]

    def rev(t):
        return jnp.flip(t, axis=2)

    h = mlstm_direction(jnp.concatenate([q, rev(q)], 0), jnp.concatenate([k, rev(k)], 0), jnp.concatenate([v, rev(v)], 0), jnp.concatenate([ig_f, rev(ig_b)], 0), jnp.concatenate([fg_f, rev(fg_b)], 0))
    h = (h[:B] + rev(h[B:])).transpose(0, 2, 1, 3)
    h = h * lax.rsqrt(jnp.mean(h * h, axis=-1, keepdims=True) + EPS) * norm_g.astype(jnp.float32).reshape(HEADS_C, HEAD_DIM)
    return (jax.nn.sigmoid(oc.astype(jnp.float32)) * h.reshape(B, S, HEADS_C * HEAD_DIM)).astype(dt)


def setup_inputs(seed: int = 0) -> dict:
    key = jax.random.key(seed)
    ks = jax.random.split(key, 16)
    f32 = jnp.float32

    def nrm(k, shape, s):
        return jax.random.normal(k, shape, f32) * s

    fb = jnp.linspace(FORGET_BIAS_LO, FORGET_BIAS_HI, HEADS_C, dtype=f32)
    zb = jnp.zeros((HEADS_C,), f32)
    return {
        'x': nrm(ks[0], (BATCH, SEQ, D_MODEL), 1.0),
        'p': nrm(ks[1], (DEPTH, BATCH, SEQ, PLE_DIM), 1.0),
        'rel_bias': nrm(ks[2], (N_BUCKETS, HEADS_A + HEADS_B), 0.2),
        'attn_norm': 1.0 + nrm(ks[3], (DEPTH, D_MODEL), 0.05),
        'w_in': nrm(ks[4], (DEPTH, D_MODEL, D_IN), D_MODEL ** -0.5),
        'qk_conv': nrm(ks[5], (DEPTH, CONV_WIDTH, 2 * HEADS_C * HEAD_DIM), CONV_WIDTH ** -0.5),
        'gate_bias': jnp.concatenate([zb, fb, zb, fb])[None] + nrm(ks[6], (DEPTH, 4 * HEADS_C), 0.1),
        'sink_logits': nrm(ks[7], (DEPTH, HEADS_B), 0.5),
        'mlstm_norm': 1.0 + nrm(ks[8], (DEPTH, HEADS_C * HEAD_DIM), 0.05),
        'w_out': nrm(ks[9], (DEPTH, D_MODEL, D_MODEL), D_MODEL ** -0.5),
        'ffn_norm': 1.0 + nrm(ks[10], (DEPTH, D_MODEL), 0.05),
        'w_up': nrm(ks[11], (DEPTH, D_MODEL, 2 * D_FF), D_MODEL ** -0.5),
        'w_down': nrm(ks[12], (DEPTH, D_FF, D_MODEL), D_FF ** -0.5),
        'ple_proj': nrm(ks[13], (DEPTH, PLE_DIM, D_MODEL), PLE_DIM ** -0.5),
        'ple_gate': nrm(ks[14], (DEPTH, D_MODEL, D_MODEL), D_MODEL ** -0.5),
        'final_norm': 1.0 + nrm(ks[15], (D_MODEL,), 0.05),
    }


def reference(x, p, rel_bias, attn_norm, w_in, qk_conv, gate_bias, sink_logits, mlstm_norm, w_out, ffn_norm, w_up, w_down, ple_proj, ple_gate, final_norm):
    B, S, _ = x.shape
    bias_a = rel_bias[:, :HEADS_A]
    bias_b = rel_bias[:, HEADS_A:]
    for i in range(DEPTH):
        h = rmsnorm(x, attn_norm[i])
        z = h @ w_in[i]
        parts = []
        off = 0
        for sz in COL_SIZES:
            parts.append(z[..., off:off + sz])
            off += sz
        qa, ka, va, qb, kb, vb, qc, kc, vc, oc, gc = parts
        ya = dilated_attention(qa.reshape(B, S, HEADS_A, HEAD_DIM), ka.reshape(B, S, HEADS_A, HEAD_DIM), va.reshape(B, S, HEADS_A, HEAD_DIM), bias_a)
        yb, _ = banded_attention(qb.reshape(B, S, HEADS_B, HEAD_DIM), kb.reshape(B, S, KV_HEADS_B, HEAD_DIM), vb.reshape(B, S, KV_HEADS_B, HEAD_DIM), bias_b, WINDOW_B, WINDOW_B, 1, sink=sink_logits[i])
        yc = mlstm_mixer(qc, kc, vc, oc, gc, qk_conv[i], gate_bias[i], mlstm_norm[i])
        y = jnp.concatenate([ya.reshape(B, S, HEADS_A * HEAD_DIM).astype(x.dtype), yb.reshape(B, S, HEADS_B * HEAD_DIM).astype(x.dtype), yc], axis=-1)
        x = x + y @ w_out[i]
        h = rmsnorm(x, ffn_norm[i])
        gu = h @ w_up[i]
        x = x + (jax.nn.silu(gu[..., :D_FF]) * gu[..., D_FF:]) @ w_down[i]
        x = x + (p[i] @ ple_proj[i]) * jax.nn.sigmoid(x @ ple_gate[i])
    return rmsnorm(x, final_norm)
```

```python
import math
import os
import numpy as np
import concourse.bass as bass
import concourse.mybir as mybir
from concourse.bass_utils import run_bass_kernel_spmd

F32 = mybir.dt.float32
BF16 = mybir.dt.bfloat16
ALU = mybir.AluOpType
AF = mybir.ActivationFunctionType

D = 1024
DEPTH = 4
NHA, NHB, NKVB, NHC = 6, 4, 2, 6
DFF = 2816
PLE = 256
DIN = 3224
EPS = 1e-6
PADK = 1024
NTAB = 512
NEGB = -30000.0

ENGS = ("pe", "act", "dve", "pool", "sp")
CH = 30000
NDS = 48


class Res:
    __slots__ = ("writers", "readers", "prev")

    def __init__(self):
        self.writers = []
        self.readers = []
        self.prev = []


class Sched:
    def __init__(self, nc):
        self.nc = nc
        self.ops = {e: [] for e in ENGS}
        self.ndma = 0
        self.pending = {e: [] for e in ENGS}
        self.live_dma = []

    def _deps(self, eng, reads, writes, parts):
        deps = []
        for r in reads:
            deps.extend(r.writers)
        for w in writes:
            deps.extend(w.readers)
            deps.extend(w.writers)
            deps.extend(w.prev)
        for w in parts:
            if w.readers:
                w.prev = w.writers + w.readers
                w.writers = []
                w.readers = []
            deps.extend(w.prev)
        if self.pending[eng]:
            deps.extend(self.pending[eng])
            self.pending[eng] = []
        return deps

    def _commit(self, tok, reads, writes, parts):
        for r in reads:
            r.readers.append(tok)
        for w in writes:
            w.writers = [tok]
            w.readers = []
            w.prev = [tok]
        for w in parts:
            w.writers.append(tok)

    def op(self, eng, meth, kw, reads=(), writes=(), parts=()):
        deps = self._deps(eng, reads, writes, parts)
        self.ops[eng].append({"fn": (meth, kw), "deps": deps, "sig": False, "dma": None})
        tok = ("e", eng, len(self.ops[eng]) - 1)
        self._commit(tok, reads, writes, parts)
        return tok

    def dma(self, eng, kw, reads=(), writes=(), parts=()):
        deps = self._deps(eng, reads, writes, parts)
        i = self.ndma
        self.ndma += 1
        if i >= NDS:
            deps.append(("d", i - NDS))
        self.ops[eng].append({"fn": ("dma_start", kw), "deps": deps, "sig": False, "dma": i})
        tok = ("d", i)
        self.live_dma.append(tok)
        if len(self.live_dma) > NDS:
            self.live_dma = self.live_dma[-NDS:]
        self._commit(tok, reads, writes, parts)
        return tok

    def all_tokens(self):
        toks = []
        for e in ENGS:
            for k in range(len(self.ops[e]) - 1, -1, -1):
                if self.ops[e][k]["dma"] is None:
                    toks.append(("e", e, k))
                    break
        toks.extend(self.live_dma)
        return toks

    def barrier(self):
        toks = []
        for e in ENGS:
            if self.ops[e]:
                for k in range(len(self.ops[e]) - 1, -1, -1):
                    if self.ops[e][k]["dma"] is None:
                        toks.append(("e", e, k))
                        break
        toks.extend(self.live_dma)
        for e in ENGS:
            self.pending[e] = list(self.pending[e]) + toks

    def emit(self, final_tokens=()):
        nc = self.nc
        for e in ENGS:
            for rec in self.ops[e]:
                for t in rec["deps"]:
                    if t[0] == "e":
                        self.ops[t[1]][t[2]]["sig"] = True
        for t in final_tokens:
            if t[0] == "e":
                self.ops[t[1]][t[2]]["sig"] = True
        nsig = {}
        for e in ENGS:
            c = 0
            for rec in self.ops[e]:
                if rec["sig"]:
                    c += 1
                    rec["cnt"] = c
            nsig[e] = c
        esems = {e: [nc.alloc_semaphore(f"s_{e}_{j}") for j in range(max(1, (nsig[e] + CH - 1) // CH))]
                 for e in ENGS}
        dsems = [nc.alloc_semaphore(f"s_dma_{j}") for j in range(min(NDS, max(1, self.ndma)))]
        ops = self.ops

        def tok_wait(t):
            if t[0] == "e":
                c = ops[t[1]][t[2]]["cnt"]
                return (("e", t[1], (c - 1) // CH), esems[t[1]][(c - 1) // CH], (c - 1) % CH + 1)
            i = t[1]
            return (("d", i % NDS), dsems[i % NDS], 16 * (i // NDS + 1))

        self.nwaits = 0
        with nc.Block() as block:
            def run(ename):
                def body(eng):
                    known = {}
                    for rec in ops[ename]:
                        need = {}
                        for t in rec["deps"]:
                            if t[0] == "e" and t[1] == ename and ename == "pe":
                                continue
                            key, sem, val = tok_wait(t)
                            if known.get(key, 0) >= val:
                                continue
                            if key not in need or need[key][1] < val:
                                need[key] = (sem, val)
                        for key, (sem, val) in need.items():
                            eng.wait_ge(sem, val)
                            known[key] = val
                            self.nwaits += 1
                        meth, kw = rec["fn"]
                        ins = getattr(eng, meth)(**kw)
                        if rec["dma"] is not None:
                            ins.then_inc(dsems[rec["dma"] % NDS], 16)
                        elif rec["sig"]:
                            c = rec["cnt"]
                            ins.then_inc(esems[ename][(c - 1) // CH], 1)
                    if ename == "sp":
                        for t in final_tokens:
                            key, sem, val = tok_wait(t)
                            if known.get(key, 0) < val:
                                eng.wait_ge(sem, val)
                                known[key] = val
                return body
            block.tensor(run("pe"))
            block.scalar(run("act"))
            block.vector(run("dve"))
            block.gpsimd(run("pool"))
            block.sync(run("sp"))


def _dsize(dt):
    return 2 if dt == BF16 else 4


class Arena:
    def __init__(self, nc, limit=229376):
        self.nc = nc
        self.off = 16640
        self.limit = limit
        self.n = 0

    def alloc(self, shape, dtype, name="t"):
        nbytes = int(np.prod(shape[1:])) * _dsize(dtype)
        nbytes = (nbytes + 63) // 64 * 64
        assert self.off + nbytes <= self.limit, (name, self.off, nbytes)
        self.n += 1
        t = self.nc.alloc_sbuf_tensor_at(f"{name}_{self.n}", list(shape), dtype, offset=self.off)
        self.off += nbytes
        return t


def _t5_bucket(rel):
    rel = np.asarray(rel, dtype=np.int64)
    n = np.abs(rel)
    nf = np.maximum(n, 1).astype(np.float32)
    large = 8 + (np.log(nf / np.float32(8)) / np.float32(math.log(1024 / 8)) * np.float32(8)).astype(np.int32)
    large = np.minimum(large, 15)
    return np.where(rel > 0, 16, 0) + np.where(n < 8, n, large)


def _onehots():
    oh = np.zeros((4, 33, NTAB), np.float32)
    m = np.arange(NTAB)
    rel = np.where(m <= NTAB // 2, -m, NTAB - m)
    for ti, (dil, band) in enumerate(((1, 64), (4, 64), (16, 64), (1, 128))):
        b = _t5_bucket(rel * dil)
        inb = np.abs(rel) <= band
        for j in range(NTAB):
            if inb[j]:
                oh[ti, b[j], j] = 1.0
            else:
                oh[ti, 32, j] = 1.0
    return oh


def build(S=8192, depth=DEPTH, dbg=(), upto=None):
    nc = bass.Bass("TRN2", target_bir_lowering=False)
    s = Sched(nc)
    NT512 = S // 512
    NT256 = S // 256
    NT128 = S // 128
    NSB = S // 2048
    SP = S + 2 * PADK

    def dram_in(name, shape, dt=F32):
        return nc.dram_tensor(name, list(shape), dt, kind="ExternalInput")

    def dram_scr(name, shape, dt):
        return nc.dram_tensor(name, list(shape), dt, kind=("ExternalOutput" if name in dbg else "Internal"))

    xT_in = dram_in("xT", [D, S])
    pT_in = dram_in("pT", [depth, PLE, S])
    relb_in = dram_in("relb", [32, 10])
    gA_in = dram_in("gA", [128, depth, 8])
    gF_in = dram_in("gF", [128, depth, 8])
    gfin_in = dram_in("gfin", [128, 8])
    gM_in = dram_in("gM", [128, depth, 3])
    cw_in = dram_in("cw", [128, depth, 6, 5])
    gb_in = dram_in("gb", [128, depth, 24])
    sk_in = dram_in("sk", [128, depth, 4])
    win_in = dram_in("w_in", [depth, D, DIN])
    wout_in = dram_in("w_out", [depth, D, D])
    wup_in = dram_in("w_up", [depth, D, 2 * DFF])
    wdn_in = dram_in("w_down", [depth, DFF, D])
    wpp_in = dram_in("ple_proj", [depth, PLE, D])
    wpg_in = dram_in("ple_gate", [depth, D, D])
    oh_in = dram_in("oh", [4, 33, NTAB])
    ident_in = dram_in("ident", [128, 128])
    mF_in = dram_in("maskF", [128, 128])
    mB_in = dram_in("maskB", [128, 128])
    sel_in = dram_in("sel", [36, 18, 128])
    out_d = nc.dram_tensor("outT", [D, S], F32, kind="ExternalOutput")

    XT = dram_scr("XT", [D, S], F32)
    ZT = dram_scr("ZT", [2304, SP], BF16)
    GT = dram_scr("GT", [24, S], F32)
    VT = dram_scr("VT", [SP, 14, 128], BF16)
    YT = dram_scr("YT", [D, S], BF16)
    MT = dram_scr("MT", [18 * 128, S], BF16)
    EG = dram_scr("EG", [36, S], F32)
    DECD = dram_scr("DECD", [2, NT128, 6], F32)
    MTK = dram_scr("MTK", [2, S, 3, 128], BF16)
    HD = dram_scr("HD", [2, 384, S], F32)
    TP = dram_scr("TP", [22, 128, NTAB], BF16)

    R = {}

    def res(*key):
        if key not in R:
            R[key] = Res()
        return R[key]

    ar = Arena(nc)
    ones_bf = ar.alloc([128, 128], BF16, "ones")
    bd_bf = ar.alloc([128, 128], BF16, "bd")
    ident_bf = ar.alloc([128, 128], BF16, "ident")
    maskF = ar.alloc([128, 128], F32, "maskF")
    maskB = ar.alloc([128, 128], F32, "maskB")
    EMA = ar.alloc([128, 3, 6, 256], BF16, "EMA")
    EMB = ar.alloc([128, 4, 384], BF16, "EMB")
    gA = ar.alloc([128, depth, 8], F32, "gA")
    gF = ar.alloc([128, depth, 8], F32, "gF")
    gfin = ar.alloc([128, 8], F32, "gfin")
    gM = ar.alloc([128, depth, 3], F32, "gM")
    cw = ar.alloc([128, depth, 6, 5], F32, "cw")
    gb = ar.alloc([128, depth, 24], F32, "gb")
    ES = ar.alloc([128, depth, 4], F32, "ES")
    zeros_bf = ar.alloc([128, 1792], BF16, "zeros")
    r_const = Res()
    PERSIST = ar.off

    PSB = [nc.alloc_psum_tensor(f"psb{i}", [128, 512], F32) for i in range(8)]
    rPS = [Res() for _ in range(8)]
    STG = {}

    def phase():
        s.barrier()
        ar.off = PERSIST
        STG.clear()

    def dma(kw, reads=(), writes=(), parts=(), q="sp"):
        return s.dma(q, kw, reads=reads, writes=writes, parts=parts)

    cast_rr = [0]

    def cast_copy(out_ap, in_ap, reads, writes=(), parts=(), engs=("act", "dve")):
        e = engs[cast_rr[0] % len(engs)]
        cast_rr[0] += 1
        if e == "act":
            s.op("act", "activation", dict(out=out_ap, in_=in_ap, func=AF.Copy), reads=reads, writes=writes, parts=parts)
        else:
            s.op(e, "tensor_copy", dict(out=out_ap, in_=in_ap), reads=reads, writes=writes, parts=parts)

    def load_weight(dst, rdst, src2d, K, N):
        CW = 1024
        if "b" not in STG:
            STG["b"] = [ar.alloc([128, CW], F32, "wstg") for _ in range(4)]
            STG["r"] = [Res() for _ in range(4)]
            STG["i"] = 0
        for k in range(K // 128):
            for c0 in range(0, N, CW):
                n = min(CW, N - c0)
                sl = STG["i"] % 4
                STG["i"] += 1
                st, rs = STG["b"][sl], STG["r"][sl]
                dma(dict(out=st[:, 0:n], in_=src2d[k * 128:(k + 1) * 128, c0:c0 + n]), writes=[rs])
                cast_copy(dst[:, k, c0:c0 + n], st[:, 0:n], reads=[rs], parts=[rdst])

    def rmsnorm_tile(XS, rXS, HT, rHT, gtile_ap_fn, T, SQ, rSQ, RS, rRS, bank):
        s.op("act", "activation", dict(out=SQ[:, :, 0:T], in_=XS[:, :, 0:T], func=AF.Square), reads=[rXS], writes=[rSQ])
        for c in range(8):
            s.op("pe", "matmul", dict(out=PSB[bank][:, 0:T], lhsT=ones_bf[:], rhs=SQ[:, c, 0:T], start=(c == 0), stop=(c == 7)),
                 reads=[rSQ, r_const], writes=[rPS[bank]])
        s.op("act", "activation", dict(out=RS[:, 0:T], in_=PSB[bank][:, 0:T], func=AF.Ln, scale=1.0 / D, bias=EPS),
             reads=[rPS[bank]], writes=[rRS])
        s.op("act", "activation", dict(out=RS[:, 0:T], in_=RS[:, 0:T], func=AF.Exp, scale=-0.5), reads=[rRS], writes=[rRS])
        for c in range(8):
            s.op("dve", "scalar_tensor_tensor", dict(out=HT[:, c, 0:T], in0=XS[:, c, 0:T], scalar=gtile_ap_fn(c), in1=RS[:, 0:T], op0=ALU.mult, op1=ALU.mult),
                 reads=[rXS, rRS, r_const], parts=[rHT])

    def phase_init():
        stg = ar.alloc([128, 128], F32, "istg")
        rstg = Res()
        s.op("dve", "memset", dict(ap=ones_bf[:], constant=1.0), writes=[r_const])
        r_bd = Res()
        s.op("dve", "memset", dict(ap=bd_bf[:], constant=0.0), writes=[r_bd])
        s.op("dve", "memset", dict(ap=bd_bf[0:64, 0:64], constant=1.0), writes=[r_bd])
        s.op("dve", "memset", dict(ap=bd_bf[64:128, 64:128], constant=1.0), writes=[r_bd])
        s.op("dve", "memset", dict(ap=zeros_bf[:], constant=0.0), parts=[r_const])
        dma(dict(out=stg[:], in_=ident_in.ap()), writes=[rstg])
        s.op("dve", "tensor_copy", dict(out=ident_bf[:], in_=stg[:]), reads=[rstg], parts=[r_const])
        for (t, src) in ((maskF, mF_in), (maskB, mB_in), (gA, gA_in), (gF, gF_in), (gfin, gfin_in), (gM, gM_in),
                         (cw, cw_in), (gb, gb_in), (ES, sk_in)):
            dma(dict(out=t[:], in_=src.ap()), parts=[r_const])
        s.op("act", "activation", dict(out=ES[:], in_=ES[:], func=AF.Exp), reads=[r_const], parts=[r_const])
        for b in range(18):
            for c0 in (0, PADK + S):
                dma(dict(out=ZT.ap()[b * 128:(b + 1) * 128, c0:c0 + PADK], in_=zeros_bf[:, 0:PADK]),
                    reads=[r_const], parts=[res("ZTpad")])
        for r0 in list(range(0, PADK, 128)) + list(range(PADK + S, SP, 128)):
            dma(dict(out=VT.ap()[r0:r0 + 128].rearrange("p h e -> p (h e)"), in_=zeros_bf[:, 0:1792]),
                reads=[r_const], parts=[res("VTpad")])
        BX = ar.alloc([128, 10], F32, "BX")
        OH = ar.alloc([128, 4, NTAB], F32, "OH")
        rBX, rOH = Res(), Res()
        s.op("dve", "memset", dict(ap=BX[32:64, :], constant=NEGB), writes=[rBX])
        dma(dict(out=BX[0:32, :], in_=relb_in.ap()), parts=[rBX])
        dma(dict(out=OH[0:33, :, :], in_=oh_in.ap().rearrange("t b n -> b t n")), writes=[rOH])
        TS = [ar.alloc([128, NTAB], BF16, "TS") for _ in range(2)]
        rTS = [Res(), Res()]
        tabs = [(pi, h, h, pi * 6 + h) for pi in range(3) for h in range(6)] + [(3, h, 6 + h, 18 + h) for h in range(4)]
        for i, (ti, h, col, tab) in enumerate(tabs):
            bk = i % 2
            s.op("pe", "matmul", dict(out=PSB[bk][:, :], lhsT=BX[0:33, col:col + 1].to_broadcast([33, 128]), rhs=OH[0:33, ti, :], start=True, stop=True),
                 reads=[rBX, rOH], writes=[rPS[bk]])
            s.op("act", "activation", dict(out=TS[bk][:], in_=PSB[bk][:, :], func=AF.Copy, scale=8.0), reads=[rPS[bk]], writes=[rTS[bk]])
            dma(dict(out=TP.ap()[tab], in_=TS[bk][:]), reads=[rTS[bk]], writes=[res("TP", tab)])
            if ti < 3:
                for c, off in ((0, 64), (1, NTAB - 64)):
                    dma(dict(out=EMA[:, ti, h, c * 128:(c + 1) * 128], in_=bass.AP(TP, tab * 128 * NTAB + off, [[NTAB - 1, 128], [1, 128]])),
                        reads=[res("TP", tab)], parts=[r_const])
            else:
                for c, off in ((0, 128), (1, 0), (2, NTAB - 128)):
                    dma(dict(out=EMB[:, h, c * 128:(c + 1) * 128], in_=bass.AP(TP, tab * 128 * NTAB + off, [[NTAB - 1, 128], [1, 128]])),
                        reads=[res("TP", tab)], parts=[r_const])

    FMB = 18

    def inproj_tile(t, XS, rXS, Win, rWin, l, bufs, part="all"):
        (HT, rHT, SQ, rSQ, RS, rRS, ZS, rZS, GS, rGS, VS, rVS) = bufs
        t0 = t * 512
        if part in ("all", "norm"):
            rmsnorm_tile(XS, rXS, HT, rHT, lambda c: gA[:, l, c:c + 1], 512, SQ, rSQ, RS, rRS, 0)
        if part == "norm":
            return
        for b in range(FMB):
            bk = 1 + (b % 2)
            for k in range(8):
                s.op("pe", "matmul", dict(out=PSB[bk][:, :], lhsT=Win[:, k, b * 128:(b + 1) * 128], rhs=HT[:, k, :], start=(k == 0), stop=(k == 7)),
                     reads=[rHT, rWin], writes=[rPS[bk]])
            cast_copy(ZS[:, b, :], PSB[bk][:, :], reads=[rPS[bk]], parts=[rZS], engs=("act", "dve"))
        dma(dict(out=ZT.ap()[:, PADK + t0:PADK + t0 + 512].rearrange("(b p) t -> p b t", p=128), in_=ZS[:]),
            reads=[rZS], writes=[res("ZT", t)])
        for k in range(8):
            s.op("pe", "matmul", dict(out=PSB[3][0:24, :], lhsT=Win[:, k, 2304:2328], rhs=HT[:, k, :], start=(k == 0), stop=(k == 7)),
                 reads=[rHT, rWin], writes=[rPS[3]])
        s.op("dve", "tensor_copy", dict(out=GS[0:24, :], in_=PSB[3][0:24, :]), reads=[rPS[3]], writes=[rGS])
        dma(dict(out=GT.ap()[:, t0:t0 + 512], in_=GS[0:24, :]), reads=[rGS], writes=[res("GT", t)])
        for sub in range(4):
            bkA, bkB = 4 + 2 * (sub % 2), 5 + 2 * (sub % 2)
            tok = slice(sub * 128, (sub + 1) * 128)
            for k in range(8):
                s.op("pe", "matmul", dict(out=PSB[bkA][:, :], lhsT=HT[:, k, tok], rhs=Win[:, k, 2328:2840], start=(k == 0), stop=(k == 7)),
                     reads=[rHT, rWin], writes=[rPS[bkA]])
            for k in range(8):
                s.op("pe", "matmul", dict(out=PSB[bkB][:, 0:384], lhsT=HT[:, k, tok], rhs=Win[:, k, 2840:3224], start=(k == 0), stop=(k == 7)),
                     reads=[rHT, rWin], writes=[rPS[bkB]])
            vs, rvs = VS[sub % 2], rVS[sub % 2]
            s.op("act", "activation", dict(out=vs[:, 0:8, 0:64], in_=PSB[bkA][:, :].rearrange("p (h e) -> p h e", e=64), func=AF.Copy), reads=[rPS[bkA]], parts=[rvs])
            s.op("dve", "tensor_copy", dict(out=vs[:, 8:14, 0:64], in_=PSB[bkB][:, 0:384].rearrange("p (h e) -> p h e", e=64)),
                 reads=[rPS[bkB]], parts=[rvs])
            r0 = PADK + t0 + sub * 128
            dma(dict(out=VT.ap()[r0:r0 + 128], in_=vs[:]), reads=[rvs], writes=[res("VT", t, sub)])

    def inproj_bufs():
        HT = ar.alloc([128, 8, 512], BF16, "HT"); SQ = ar.alloc([128, 8, 512], BF16, "SQ")
        RS = ar.alloc([128, 512], F32, "RS"); ZS = ar.alloc([128, FMB, 512], BF16, "ZS")
        GS = ar.alloc([128, 512], F32, "GS")
        VS = [ar.alloc([128, 14, 128], BF16, "VS") for _ in range(2)]
        rVS = [Res(), Res()]
        for v, rv in zip(VS, rVS):
            s.op("pool", "memset", dict(ap=v[:], constant=1.0), writes=[rv])
        return (HT, Res(), SQ, Res(), RS, Res(), ZS, Res(), GS, Res(), VS, rVS)

    def load_win(l):
        Win = ar.alloc([128, 8, DIN], BF16, "Win")
        rWin = Res()
        load_weight(Win, rWin, win_in.ap()[l], D, DIN)
        return Win, rWin

    def phase_inproj0():
        phase()
        Win, rWin = load_win(0)
        bufs = inproj_bufs()
        XSs = [ar.alloc([128, 8, 512], F32, "XS") for _ in range(2)]
        rXSs = [Res(), Res()]

        def ld(t):
            dma(dict(out=XSs[t % 2][:], in_=xT_in.ap()[:, t * 512:(t + 1) * 512].rearrange("(c p) t -> p c t", p=128)),
                writes=[rXSs[t % 2]])
        ld(0)
        for t in range(NT512):
            if t + 1 < NT512:
                ld(t + 1)
            dma(dict(out=XT.ap()[:, t * 512:(t + 1) * 512].rearrange("(c p) t -> p c t", p=128), in_=XSs[t % 2][:]),
                reads=[rXSs[t % 2]], writes=[res("XT", t)])
            inproj_tile(t, XSs[t % 2], rXSs[t % 2], Win, rWin, 0, bufs)

    def phase_gates(l):
        phase()
        NC_ = NT128
        G = ar.alloc([128, 24, 128], F32, "G")
        L = ar.alloc([128, 24, 128], F32, "L")
        CA = ar.alloc([128, 12, 128], F32, "CA")
        CB = ar.alloc([128, 12, 128], F32, "CB")
        EGS = ar.alloc([128, 36, 128], F32, "EGS")
        DC = ar.alloc([128, 12], F32, "DC")
        rG, rL, rCA, rCB, rEGS, rDC = [Res() for _ in range(6)]
        P = slice(0, NC_)
        dma(dict(out=G[P], in_=GT.ap().rearrange("g (c s) -> c g s", s=128)),
            reads=[res("GT", t) for t in range(NT512)], writes=[rG])
        s.op("dve", "tensor_tensor", dict(out=G[P], in0=G[P], in1=gb[P, l, :].unsqueeze(2).to_broadcast([NC_, 24, 128]), op=ALU.add),
             reads=[rG, r_const], writes=[rG])
        s.op("act", "activation", dict(out=L[P], in_=G[P], func=AF.Exp, scale=-1.0), reads=[rG], writes=[rL])
        s.op("act", "activation", dict(out=CA[P, 0:6, :], in_=L[P, 6:12, :], func=AF.Ln, bias=1.0), reads=[rL], writes=[rCA])
        s.op("act", "activation", dict(out=CA[P, 6:12, :], in_=L[P, 18:24, :], func=AF.Ln, bias=1.0), reads=[rL], writes=[rCA])
        src, rsrc, dst, rdst = CA, rCA, CB, rCB
        k = 1
        while k < 128:
            s.op("dve", "tensor_tensor", dict(out=dst[P, 0:6, k:128], in0=src[P, 0:6, k:128], in1=src[P, 0:6, 0:128 - k], op=ALU.add),
                 reads=[rsrc], writes=[rdst])
            s.op("dve", "tensor_copy", dict(out=dst[P, 0:6, 0:k], in_=src[P, 0:6, 0:k]), reads=[rsrc], parts=[rdst])
            s.op("pool", "tensor_tensor", dict(out=dst[P, 6:12, 0:128 - k], in0=src[P, 6:12, 0:128 - k], in1=src[P, 6:12, k:128], op=ALU.add),
                 reads=[rsrc], parts=[rdst])
            s.op("pool", "tensor_copy", dict(out=dst[P, 6:12, 128 - k:128], in_=src[P, 6:12, 128 - k:128]), reads=[rsrc], parts=[rdst])
            src, rsrc, dst, rdst = dst, rdst, src, rsrc
            k *= 2
        CS, rCS, TMP, rTMP = src, rsrc, dst, rdst
        s.op("act", "activation", dict(out=EGS[P, 0:12, :], in_=CS[P], func=AF.Exp, scale=-1.0), reads=[rCS], parts=[rEGS])
        s.op("dve", "tensor_tensor", dict(out=TMP[P, 0:6, :], in0=CS[P, 0:6, :], in1=G[P, 0:6, :], op=ALU.add), reads=[rCS, rG], writes=[rTMP])
        s.op("dve", "tensor_tensor", dict(out=TMP[P, 6:12, :], in0=CS[P, 6:12, :], in1=G[P, 12:18, :], op=ALU.add), reads=[rCS, rG], parts=[rTMP])
        s.op("act", "activation", dict(out=EGS[P, 12:24, :], in_=TMP[P], func=AF.Exp, bias=math.log(0.125)), reads=[rTMP], parts=[rEGS])
        s.op("dve", "tensor_tensor", dict(out=TMP[P, 0:6, :], in0=TMP[P, 0:6, :], in1=CS[P, 0:6, 127:128].to_broadcast([NC_, 6, 128]), op=ALU.subtract),
             reads=[rCS, rTMP], writes=[rTMP])
        s.op("dve", "tensor_tensor", dict(out=TMP[P, 6:12, :], in0=TMP[P, 6:12, :], in1=CS[P, 6:12, 0:1].to_broadcast([NC_, 6, 128]), op=ALU.subtract),
             reads=[rCS, rTMP], writes=[rTMP])
        s.op("act", "activation", dict(out=EGS[P, 24:36, :], in_=TMP[P], func=AF.Exp, bias=math.log(0.125)), reads=[rTMP], parts=[rEGS])
        s.op("act", "activation", dict(out=DC[P, 0:6], in_=CS[P, 0:6, 127], func=AF.Exp, scale=-1.0), reads=[rCS], writes=[rDC])
        s.op("act", "activation", dict(out=DC[P, 6:12], in_=CS[P, 6:12, 0], func=AF.Exp, scale=-1.0), reads=[rCS], parts=[rDC])
        dma(dict(out=EG.ap().rearrange("r (c s) -> c r s", s=128), in_=EGS[P]), reads=[rEGS], writes=[res("EG")])
        DC2 = ar.alloc([128, 2, 6], F32, "DC2"); rDC2 = Res()
        for hh in range(2):
            s.op("dve", "tensor_copy", dict(out=DC2[P, hh, :], in_=DC[P, hh:12:2]), reads=[rDC], parts=[rDC2])
        dma(dict(out=DECD.ap().rearrange("h c k -> c h k"), in_=DC2[P, :, :]), reads=[rDC2], writes=[res("DECD")])

    def phase_attnA(l):
        phase()
        QTm = [ar.alloc([128, 3, 2048], BF16, "QTm") for _ in range(2)]; rQT = Res()
        for hh in range(2):
            s.op("pool", "memset", dict(ap=QTm[hh][(1 - hh) * 64:(2 - hh) * 64, :, :], constant=0.0), parts=[rQT])
        KD = {4: ar.alloc([128, 3, 4, 1024], BF16, "KD4"), 16: ar.alloc([128, 3, 16, 256], BF16, "KD16")}
        rKD = {4: Res(), 16: Res()}
        KT = ar.alloc([128, 3, 4096], BF16, "KT"); rKT = Res()
        ACC = ar.alloc([128, 6, 2048], F32, "ACC"); rACCg = {0: Res(), 4: Res()}
        NV = 6
        VCH = [ar.alloc([128, 6, 128], BF16, "VCH") for _ in range(NV)]; rV = [Res() for _ in range(NV)]
        NE = 6
        PT_ = [ar.alloc([128, 512], BF16, "PT") for _ in range(NE)]; rP = [Res() for _ in range(NE)]
        DN = ar.alloc([128, 2048], F32, "DN"); rDN = Res()
        YA = [ar.alloc([128, 2048], BF16, "YA") for _ in range(2)]; rYA = [Res(), Res()]
        zt_all = [res("ZT", t) for t in range(NT512)]
        vt_all = [res("VT", t, sub) for t in range(NT512) for sub in range(4)] + [res("VTpad")]
        vi = [0]
        ei = [0]
        sbank = [0]
        obank = [0]
        LAG = 2
        for sb in range(NSB):
            c0 = sb * 2048
            for b in range(3):
                for hh in range(2):
                    dma(dict(out=QTm[hh][hh * 64:(hh + 1) * 64, b, :], in_=ZT.ap()[b * 128 + hh * 64:b * 128 + (hh + 1) * 64, PADK + c0:PADK + c0 + 2048]),
                        reads=zt_all, parts=[rQT])
                dma(dict(out=KT[:, b, :], in_=ZT.ap()[(3 + b) * 128:(4 + b) * 128, c0:c0 + 4096]),
                    reads=zt_all + [res("ZTpad")], parts=[rKT])
            for dd in (4, 16):
                for b in range(3):
                    en_ = ("act", "dve", "pool")[(b + (0 if dd == 4 else 1)) % 3]
                    if en_ == "act":
                        s.op("act", "activation", dict(out=KD[dd][:, b, :, :], in_=KT[:, b, :].rearrange("p (u r) -> p r u", r=dd), func=AF.Copy),
                             reads=[rKT], parts=[rKD[dd]])
                    else:
                        s.op(en_, "tensor_copy", dict(out=KD[dd][:, b, :, :], in_=KT[:, b, :].rearrange("p (u r) -> p r u", r=dd)), reads=[rKT], parts=[rKD[dd]])
            steps = []
            for pi, d in enumerate((1, 4, 16)):
                ntile = 2048 // d // 128
                for r in range(d):
                    for j0 in range(ntile):
                        steps.append((pi, d, r, j0, ntile))
            pend = []

            def emit_S(st, b):
                pi, d, r, j0 = st["pi"], st["d"], st["r"], st["j0"]
                bk = sbank[0] % 4
                sbank[0] += 1
                for hh in range(2):
                    s.op("pe", "matmul", dict(out=PSB[bk][:, hh * 256:(hh + 1) * 256], lhsT=ident_bf[:], rhs=EMA[:, pi, 2 * b + hh, :], start=True, stop=False),
                         reads=[r_const], writes=[rPS[bk]])
                    for c in range(2):
                        if d == 1:
                            k0 = 1024 + 128 * (j0 + c) - 64
                            lhs = KT[:, b, k0:k0 + 128]
                        else:
                            u0 = 128 * (j0 + c) - 64 + 1024 // d
                            lhs = KD[d][:, b, r, u0:u0 + 128]
                        s.op("pe", "matmul", dict(out=PSB[bk][:, hh * 256 + c * 128:hh * 256 + (c + 1) * 128], lhsT=lhs, rhs=QTm[hh][:, b, st["qs"]],
                                                  start=False, stop=(c == 1)), reads=[rKT, rQT] + ([rKD[d]] if d > 1 else []), writes=[rPS[bk]])
                es = ei[0] % NE
                ei[0] += 1
                s.op("act", "activation", dict(out=PT_[es][:], in_=PSB[bk][:, :], func=AF.Exp, scale=0.125), reads=[rPS[bk]], writes=[rP[es]])
                return es

            def emit_PV(st, b, es):
                h0, nh = (0, 4) if b < 2 else (4, 2)
                bk = st["ob"][0 if b < 2 else 1]
                for hh in range(2):
                    h = 2 * b + hh
                    for c in range(2):
                        s.op("pe", "matmul", dict(out=PSB[bk][:, (h - h0) * 128:(h - h0 + 1) * 128], lhsT=VCH[st["v"][c]][:, h, :],
                                                  rhs=PT_[es][:, hh * 256 + c * 128:hh * 256 + (c + 1) * 128], start=(c == 0), stop=(c == 1)),
                             reads=[rV[st["v"][c]], rP[es]], writes=[rPS[bk]])
                if b in (1, 2):
                    src_ = PSB[bk][:, 0:nh * 128].rearrange("p (h q) -> p h q", h=nh)
                    qs = st["qs"]
                    if st["pi"] == 0:
                        s.op("dve", "tensor_copy", dict(out=ACC[:, h0:h0 + nh, qs], in_=src_), reads=[rPS[bk]], writes=[rACCg[h0]])
                    else:
                        s.op("dve", "tensor_tensor", dict(out=ACC[:, h0:h0 + nh, qs], in0=src_, in1=ACC[:, h0:h0 + nh, qs], op=ALU.add),
                             reads=[rPS[bk]], writes=[rACCg[h0]])

            vslot = {}
            for (pi, d, r, j0, ntile) in steps:
                def loadv(m):
                    key = (pi, r, m)
                    if key in vslot:
                        return vslot[key]
                    sl = vi[0] % NV
                    vi[0] += 1
                    row0 = PADK + sb * 2048 + r + d * (128 * m - 64)
                    dma(dict(out=VCH[sl][:], in_=VT.ap()[row0:row0 + 127 * d + 1:d, 0:6, :]), reads=vt_all, writes=[rV[sl]])
                    vslot[key] = sl
                    return sl
                v0 = loadv(j0)
                v1 = loadv(j0 + 1)
                q0 = r + d * 128 * j0
                obp = 4 + 2 * (obank[0] % 2)
                obank[0] += 1
                st = dict(pi=pi, d=d, r=r, j0=j0, qs=slice(q0, q0 + 127 * d + 1, d), v=(v0, v1), ob=(obp, obp + 1))
                for b in range(3):
                    es = emit_S(st, b)
                    pend.append((st, b, es))
                    if len(pend) > LAG:
                        emit_PV(*pend.pop(0))
            while pend:
                emit_PV(*pend.pop(0))
            for h in range(6):
                ya, rya = YA[h % 2], rYA[h % 2]
                s.op("act", "activation", dict(out=DN[0:64, :], in_=ACC[64:128, h, :], func=AF.Ln), reads=[rACCg[0], rACCg[4]], writes=[rDN])
                s.op("act", "activation", dict(out=DN[0:64, :], in_=DN[0:64, :], func=AF.Exp, scale=-1.0), reads=[rDN], writes=[rDN])
                s.op("dve", "tensor_tensor", dict(out=ya[0:64, :], in0=ACC[0:64, h, :], in1=DN[0:64, :], op=ALU.mult),
                     reads=[rDN, rACCg[0], rACCg[4]], writes=[rya])
                dma(dict(out=YT.ap()[h * 64:(h + 1) * 64, sb * 2048:(sb + 1) * 2048], in_=ya[0:64, :]),
                    reads=[rya], writes=[res("YT", "A", h, sb)])

    def phase_attnB(l):
        phase()
        NB_ = 2
        QB = [[ar.alloc([128, 2, 512], BF16, "QB") for _ in range(2)] for _ in range(NB_)]; rQB = [Res() for _ in range(NB_)]
        for sl_ in range(NB_):
            for hh in range(2):
                s.op("pool", "memset", dict(ap=QB[sl_][hh][(1 - hh) * 64:(2 - hh) * 64, :, :], constant=0.0), parts=[rQB[sl_]])
        KB = [ar.alloc([128, 768], BF16, "KB") for _ in range(NB_)]; rKB = [Res() for _ in range(NB_)]
        VB = [ar.alloc([128, 6, 2, 128], BF16, "VB") for _ in range(NB_)]; rVB = [Res() for _ in range(NB_)]
        NE = 6
        PT_ = [ar.alloc([128, 384], BF16, "PTb") for _ in range(NE)]; rP = [Res() for _ in range(NE)]
        DN = [ar.alloc([128, 512], F32, "DNb") for _ in range(2)]; rDN = [Res(), Res()]
        YB = [ar.alloc([128, 512], BF16, "YB") for _ in range(2)]; rYB = [Res(), Res()]
        zt_all = [res("ZT", t) for t in range(NT512)] + [res("ZTpad")]
        vt_all = [res("VT", t, sub) for t in range(NT512) for sub in range(4)] + [res("VTpad")]

        def ld(g):
            sl = g % NB_
            t0 = g * 512
            for hh in range(2):
                dma(dict(out=QB[sl][hh][hh * 64:(hh + 1) * 64, :, :],
                         in_=ZT.ap()[6 * 128:8 * 128, PADK + t0:PADK + t0 + 512].rearrange("(b p) t -> p b t", p=128)[hh * 64:(hh + 1) * 64]),
                    reads=zt_all, parts=[rQB[sl]])
            dma(dict(out=KB[sl][:], in_=ZT.ap()[8 * 128:9 * 128, PADK + t0 - 128:PADK + t0 + 640]), reads=zt_all, writes=[rKB[sl]])
            dma(dict(out=VB[sl][:], in_=VT.ap()[PADK + t0 - 128:PADK + t0 + 640, 6:8, :].rearrange("(c p) h e -> p c h e", p=128)),
                reads=vt_all, writes=[rVB[sl]])
        ld(0)
        ei = [0]
        sbank = [0]
        ob = [0]
        yi = [0]
        LAG = 2
        pend = []

        def emit_S(u):
            g, h, qt, sl = u["g"], u["h"], u["qt"], u["sl"]
            qb, hh = h % 2, h // 2
            bk = sbank[0] % 4
            sbank[0] += 1
            s.op("pe", "matmul", dict(out=PSB[bk][:, 0:384], lhsT=ident_bf[:], rhs=EMB[:, h, :], start=True, stop=False), reads=[r_const], writes=[rPS[bk]])
            for c in range(3):
                s.op("pe", "matmul", dict(out=PSB[bk][:, c * 128:(c + 1) * 128], lhsT=KB[sl][:, (qt + c) * 128:(qt + c + 1) * 128],
                                          rhs=QB[sl][hh][:, qb, qt * 128:(qt + 1) * 128], start=False, stop=(c == 2)),
                     reads=[rKB[sl], rQB[sl]], writes=[rPS[bk]])
            es = ei[0] % NE
            ei[0] += 1
            s.op("act", "activation", dict(out=PT_[es][:], in_=PSB[bk][:, 0:384], func=AF.Exp, scale=0.125), reads=[rPS[bk]], writes=[rP[es]])
            u["es"] = es

        def emit_PV(u):
            g, h, qt, sl, es, bko = u["g"], u["h"], u["qt"], u["sl"], u["es"], u["bko"]
            hh = h // 2
            for c in range(3):
                s.op("pe", "matmul", dict(out=PSB[bko][:, qt * 128:(qt + 1) * 128], lhsT=VB[sl][:, qt + c, hh, :], rhs=PT_[es][:, c * 128:(c + 1) * 128],
                                          start=(c == 0), stop=(c == 2)), reads=[rVB[sl], rP[es]], writes=[rPS[bko]])
            if qt == 3:
                y = yi[0] % 2
                yi[0] += 1
                s.op("dve", "tensor_scalar", dict(out=DN[y][0:64, :], in0=PSB[bko][64:128, :], scalar1=ES[64:128, l, h:h + 1], scalar2=None, op0=ALU.add),
                     reads=[rPS[bko], r_const], writes=[rDN[y]])
                s.op("act", "activation", dict(out=DN[y][0:64, :], in_=DN[y][0:64, :], func=AF.Ln), reads=[rDN[y]], writes=[rDN[y]])
                s.op("act", "activation", dict(out=DN[y][0:64, :], in_=DN[y][0:64, :], func=AF.Exp, scale=-1.0), reads=[rDN[y]], writes=[rDN[y]])
                s.op("dve", "tensor_tensor", dict(out=YB[y][0:64, :], in0=PSB[bko][0:64, :], in1=DN[y][0:64, :], op=ALU.mult),
                     reads=[rPS[bko], rDN[y]], writes=[rYB[y]])
                dma(dict(out=YT.ap()[384 + h * 64:384 + (h + 1) * 64, g * 512:(g + 1) * 512], in_=YB[y][0:64, :]),
                    reads=[rYB[y]], writes=[res("YT", "B", h, g)])

        for g in range(NT512):
            while pend:
                emit_PV(pend.pop(0))
            if g + 1 < NT512:
                ld(g + 1)
            sl = g % NB_
            for h in range(4):
                bko = 4 + ob[0] % 4
                ob[0] += 1
                for qt in range(4):
                    u = dict(g=g, h=h, qt=qt, sl=sl, bko=bko)
                    emit_S(u)
                    pend.append(u)
                    if len(pend) > LAG:
                        emit_PV(pend.pop(0))
        while pend:
            emit_PV(pend.pop(0))

    def phase_mlstm_pre(l):
        phase()
        NBF = 2
        XC = [ar.alloc([128, 6, 516], BF16, "XC") for _ in range(NBF)]; rXC = [Res() for _ in range(NBF)]
        EGT = [ar.alloc([128, 512], F32, "EGT") for _ in range(NBF)]; rEGT = [Res() for _ in range(NBF)]
        SELf = ar.alloc([128, 18, 128], F32, "SELf"); rSELf = Res()
        SEL = ar.alloc([128, 18, 128], BF16, "SEL"); rSEL = Res()
        s.op("pool", "memset", dict(ap=SELf[:], constant=0.0), writes=[rSELf])
        dma(dict(out=SELf[0:36, :, :], in_=sel_in.ap()), writes=[rSELf])
        s.op("dve", "tensor_copy", dict(out=SEL[:], in_=SELf[:]), reads=[rSELf], writes=[rSEL])
        EGH = [ar.alloc([128, 2, 512], BF16, "EGH") for _ in range(NBF)]; rEGH = [Res() for _ in range(NBF)]
        for sl_ in range(NBF):
            s.op("pool", "memset", dict(ap=EGH[sl_][:], constant=0.0), writes=[rEGH[sl_]])
        SL = [ar.alloc([128, 512], F32, "SL") for _ in range(4)]; rSL = [Res() for _ in range(4)]
        MS = [ar.alloc([128, 18, 512], BF16, "MS") for _ in range(2)]; rMS = [Res(), Res()]
        KTS = [ar.alloc([128, 24, 128], BF16, "KTS") for _ in range(2)]; rKTS = [Res(), Res()]
        DG = ar.alloc([128, 6, 5, 128], BF16, "DG"); rDG = Res()
        zt_all = [res("ZT", t) for t in range(NT512)] + [res("ZTpad")]
        for b6 in range(6):
            for j in range(5):
                s.op("dve" if (b6 + j) % 2 == 0 else "pool", "tensor_scalar",
                     dict(out=DG[:, b6, j, :], in0=ident_bf[:], scalar1=cw[:, l, b6, j:j + 1], scalar2=None, op0=ALU.mult), reads=[r_const], parts=[rDG])

        def ld(t):
            sl = t % NBF
            t0 = t * 512
            dma(dict(out=XC[sl][:], in_=ZT.ap()[9 * 128:15 * 128, PADK + t0 - 2:PADK + t0 + 514].rearrange("(b p) t -> p b t", p=128)),
                reads=zt_all, writes=[rXC[sl]])
            dma(dict(out=EGT[sl][0:36, :], in_=EG.ap()[:, t0:t0 + 512]), reads=[res("EG")], writes=[rEGT[sl]])
        ld(0)
        bci = [0]
        ai = [0]
        pr = [0]
        for t in range(NT512):
            if t + 1 < NT512:
                ld(t + 1)
            sl = t % NBF
            ms, rms = MS[t % 2], rMS[t % 2]
            kts, rkts = KTS[t % 2], rKTS[t % 2]
            s.op("act", "activation", dict(out=EGH[sl][0:36, 0, :], in_=EGT[sl][0:36, :], func=AF.Copy), reads=[rEGT[sl]], writes=[rEGH[sl]])
            s.op("dve", "tensor_tensor", dict(out=EGT[sl][0:36, :], in0=EGT[sl][0:36, :], in1=EGH[sl][0:36, 0, :], op=ALU.subtract),
                 reads=[rEGH[sl]], writes=[rEGT[sl]])
            s.op("dve", "tensor_copy", dict(out=EGH[sl][0:36, 1, :], in_=EGT[sl][0:36, :]), reads=[rEGT[sl]], writes=[rEGH[sl]])
            for b6 in range(6):
                a = ai[0] % 4
                ai[0] += 1
                bk = a % 3
                for j in range(5):
                    s.op("pe", "matmul", dict(out=PSB[bk][:, :], lhsT=DG[:, b6, j, :], rhs=XC[sl][:, b6, j:j + 512], start=(j == 0), stop=(j == 4)),
                         reads=[rXC[sl], rDG], writes=[rPS[bk]])
                s.op("act", "activation", dict(out=SL[a][:], in_=PSB[bk][:, :], func=AF.Silu), reads=[rPS[bk]], writes=[rSL[a]])
                for dr in range(2):
                    qtys = (0,) if b6 < 3 else (1, 2)
                    for qty in qtys:
                        idx = dr * 9 + qty * 3 + (b6 % 3)
                        bb = (3, 7)[bci[0] % 2]
                        bci[0] += 1
                        s.op("pe", "matmul", dict(out=PSB[bb][:, :], lhsT=SEL[:, idx, :], rhs=EGH[sl][:, 0, :], start=True, stop=False),
                             reads=[rSEL, rEGH[sl]], writes=[rPS[bb]])
                        s.op("pe", "matmul", dict(out=PSB[bb][:, :], lhsT=SEL[:, idx, :], rhs=EGH[sl][:, 1, :], start=False, stop=True),
                             reads=[rSEL, rEGH[sl]], writes=[rPS[bb]])
                        s.op("dve", "tensor_tensor", dict(out=ms[:, idx, :], in0=SL[a][:], in1=PSB[bb][:, :], op=ALU.mult),
                             reads=[rSL[a], rPS[bb]], parts=[rms])
            dma(dict(out=MT.ap()[:, t * 512:(t + 1) * 512].rearrange("(b p) t -> p b t", p=128), in_=ms[:]),
                reads=[rms], writes=[res("MT", t)])
            for i in range(24):
                dr, rem = divmod(i, 12)
                sub, b = divmod(rem, 3)
                bk = 4 + i // 8
                pT = PSB[bk][:, :].bitcast(BF16)
                s.op("pe", "transpose", dict(out=pT[:, (i % 8) * 128:(i % 8 + 1) * 128], in_=ms[:, dr * 9 + 6 + b, sub * 128:(sub + 1) * 128], identity=ident_bf[:]),
                     reads=[rms, r_const], writes=[rPS[bk]])
                if i % 8 == 7:
                    s.op("act", "activation", dict(out=kts[:, i - 7:i + 1, :].rearrange("p a e -> p (a e)"), in_=pT[:, :], func=AF.Copy),
                         reads=[rPS[bk]], parts=[rkts])
            for dr in range(2):
                dma(dict(out=MTK.ap()[dr, t * 512:(t + 1) * 512].rearrange("(sub p) b e -> p sub b e", p=128),
                         in_=kts[:, dr * 12:(dr + 1) * 12, :].rearrange("p (sub b) e -> p sub b e", b=3)),
                    reads=[rkts], writes=[res("MTK", dr, t)])

    def phase_mlstm_rec(l):
        phase()
        NBF = 2
        QK = [[ar.alloc([128, 6, 128], BF16, "QK") for _ in range(NBF)] for _ in range(2)]
        rQK = [[Res() for _ in range(NBF)] for _ in range(2)]
        VV = [[ar.alloc([128, 6, 128], BF16, "VV") for _ in range(NBF)] for _ in range(2)]
        QM = [[[ar.alloc([128, 3, 128], BF16, "QM") for _ in range(2)] for _ in range(NBF)] for _ in range(2)]
        KMt = [[ar.alloc([128, 2, 3, 128], BF16, "KM") for _ in range(NBF)] for _ in range(2)]
        KM = [[[KMt[d_][s_][:, hh_, :, :] for hh_ in range(2)] for s_ in range(NBF)] for d_ in range(2)]
        rQKq = [[Res() for _ in range(NBF)] for _ in range(2)]
        for dr_ in range(2):
            for sl_ in range(NBF):
                for hh in range(2):
                    s.op("pool", "memset", dict(ap=QM[dr_][sl_][hh][(1 - hh) * 64:(2 - hh) * 64, :, :], constant=0.0), parts=[rQK[dr_][sl_]])
                    s.op("pool", "memset", dict(ap=KMt[dr_][sl_][:, hh, :, (1 - hh) * 64:(2 - hh) * 64], constant=0.0), parts=[rQK[dr_][sl_]])
        C32 = [ar.alloc([128, 3, 128], F32, "C32") for _ in range(2)]
        Cbf = [ar.alloc([128, 3, 128], BF16, "Cbf") for _ in range(2)]
        rC32 = [Res(), Res()]
        rCbf = [Res(), Res()]
        DECH = ar.alloc([128, NT128, 6], F32, "DECH"); rDECH = Res()
        PM = [ar.alloc([128, 6, 128], BF16, "PM") for _ in range(2)]; rPM = [Res(), Res()]
        AB = [ar.alloc([128, 512], F32, "AB") for _ in range(4)]; rAB = [Res() for _ in range(4)]
        HS = [ar.alloc([128, 6, 128], F32, "HS") for _ in range(2)]; rHS = [Res(), Res()]
        mt_all = [res("MT", t) for t in range(NT512)]
        mtk_all = [res("MTK", dr, t) for dr in range(2) for t in range(NT512)]
        vt_all = [res("VT", t, sub) for t in range(NT512) for sub in range(4)]
        for hh in range(2):
            dma(dict(out=DECH[hh * 64:(hh + 1) * 64, :, :].rearrange("p c k -> p (c k)"), in_=bass.AP(DECD, hh * NT128 * 6, [[0, 64], [1, NT128 * 6]])),
                reads=[res("DECD")], parts=[rDECH])
        for dr in range(2):
            s.op("dve", "memset", dict(ap=C32[dr][:], constant=0.0), writes=[rC32[dr]])
            s.op("pool", "memset", dict(ap=Cbf[dr][:], constant=0.0), writes=[rCbf[dr]])

        def tile_of(dr, g):
            return g if dr == 0 else NT128 - 1 - g

        def ld(dr, g):
            sl = g % NBF
            j = tile_of(dr, g)
            r = rQK[dr][sl]
            mtv = MT.ap()[dr * 9 * 128:(dr + 1) * 9 * 128, j * 128:(j + 1) * 128].rearrange("(b p) t -> p b t", p=128)
            dma(dict(out=QK[dr][sl][:], in_=mtv[:, 0:6, :]), reads=mt_all, parts=[r], writes=[rQKq[dr][sl]])
            for hh in range(2):
                s.op("pool", "tensor_copy", dict(out=QM[dr][sl][hh][hh * 64:(hh + 1) * 64, :, :], in_=QK[dr][sl][hh * 64:(hh + 1) * 64, 0:3, :]),
                     reads=[rQKq[dr][sl]], parts=[r])
            for hh in range(2):
                dma(dict(out=KMt[dr][sl][:, hh, :, hh * 64:(hh + 1) * 64], in_=MTK.ap()[dr, j * 128:(j + 1) * 128, :, hh * 64:(hh + 1) * 64]),
                    reads=mtk_all, parts=[r])
            dma(dict(out=VV[dr][sl][:], in_=VT.ap()[PADK + j * 128:PADK + (j + 1) * 128, 8:14, :]), reads=vt_all, parts=[r])
        for dr in range(2):
            ld(dr, 0)
        for g in range(NT128):
            for dr in range(2):
                if g + 1 < NT128:
                    ld(dr, g + 1)
            sl = g % NBF
            for dr in range(2):
                qk, rqk, vv, qm, km = QK[dr][sl], rQK[dr][sl], VV[dr][sl], QM[dr][sl], KM[dr][sl]
                bA, bB, bD, bE = 4 * dr, 4 * dr + 1, 4 * dr + 2, 4 * dr + 3
                for h in range(6):
                    b, hh = h // 2, h % 2
                    bk, c0 = (bA, h * 128) if h < 4 else (bB, (h - 4) * 128)
                    s.op("pe", "matmul", dict(out=PSB[bk][:, c0:c0 + 128], lhsT=qk[:, 3 + b, :], rhs=qm[hh][:, b, :], start=True, stop=True),
                         reads=[rqk], writes=[rPS[bk]])
                for b in range(3):
                    bk, c0 = (bB, 256 + b * 128) if b < 2 else (bE, 256)
                    for hh in range(2):
                        s.op("pe", "matmul", dict(out=PSB[bk][:, c0:c0 + 128], lhsT=km[hh][:, b, :], rhs=vv[:, 2 * b + hh, :], start=(hh == 0), stop=(hh == 1)),
                             reads=[rqk], writes=[rPS[bk]])
            for dr in range(2):
                bA, bB = 4 * dr, 4 * dr + 1
                mask = maskF if dr == 0 else maskB
                s.op("dve", "tensor_tensor", dict(out=PM[dr][:, 0:4, :], in0=PSB[bA][:, :].rearrange("p (h t) -> p h t", h=4),
                                                  in1=mask[:].unsqueeze(1).to_broadcast([128, 4, 128]), op=ALU.mult),
                     reads=[rPS[bA], r_const], writes=[rPM[dr]])
                s.op("dve", "tensor_tensor", dict(out=PM[dr][:, 4:6, :], in0=PSB[bB][:, 0:256].rearrange("p (h t) -> p h t", h=2),
                                                  in1=mask[:].unsqueeze(1).to_broadcast([128, 2, 128]), op=ALU.mult),
                     reads=[rPS[bB], r_const], parts=[rPM[dr]])
            for dr in range(2):
                rqk, vv, qm = rQK[dr][sl], VV[dr][sl], QM[dr][sl]
                bD, bE = 4 * dr + 2, 4 * dr + 3
                for h in range(6):
                    b, hh = h // 2, h % 2
                    bk, c0 = (bD, h * 128) if h < 4 else (bE, (h - 4) * 128)
                    s.op("pe", "matmul", dict(out=PSB[bk][:, c0:c0 + 128], lhsT=vv[:, h, :], rhs=PM[dr][:, h, :], start=True, stop=False),
                         reads=[rqk, rPM[dr]], writes=[rPS[bk]])
                    s.op("pe", "matmul", dict(out=PSB[bk][:, c0:c0 + 128], lhsT=Cbf[dr][:, b, :], rhs=qm[hh][:, b, :], start=False, stop=True),
                         reads=[rCbf[dr], rqk], writes=[rPS[bk]])
            items = []
            for dr in range(2):
                bD, bE = 4 * dr + 2, 4 * dr + 3
                items.append((dr, bD, 0, 4, 2 * dr))
                items.append((dr, bE, 4, 2, 2 * dr + 1))
            for dr in range(2):
                j = tile_of(dr, g)
                bB, bE = 4 * dr + 1, 4 * dr + 3
                s.op("dve", "tensor_tensor", dict(out=C32[dr][:], in0=C32[dr][:], in1=DECH[:, j, dr * 3:dr * 3 + 3].unsqueeze(2).to_broadcast([128, 3, 128]), op=ALU.mult),
                     reads=[rDECH], writes=[rC32[dr]])
                s.op("dve", "tensor_tensor", dict(out=C32[dr][:, 0:2, :], in0=PSB[bB][:, 256:512].rearrange("p (b e) -> p b e", b=2), in1=C32[dr][:, 0:2, :], op=ALU.add),
                     reads=[rPS[bB]], writes=[rC32[dr]])
                s.op("dve", "tensor_tensor", dict(out=C32[dr][:, 2, :], in0=PSB[bE][:, 256:384], in1=C32[dr][:, 2, :], op=ALU.add),
                     reads=[rPS[bE]], writes=[rC32[dr]])
            for dr in range(2):
                s.op("act", "activation", dict(out=Cbf[dr][:], in_=C32[dr][:], func=AF.Copy), reads=[rC32[0], rC32[1]], writes=[rCbf[dr]])
            for (dr, ob, h0, nh, a) in items:
                W = nh * 128
                s.op("act", "activation", dict(out=AB[a][0:64, 0:W], in_=PSB[ob][64:128, 0:W], func=AF.Abs), reads=[rPS[ob], rC32[0], rC32[1]], writes=[rAB[a]])
            for (dr, ob, h0, nh, a) in items:
                W = nh * 128
                s.op("dve", "tensor_scalar_max", dict(out=AB[a][0:64, 0:W], in0=AB[a][0:64, 0:W], scalar1=1.0), reads=[rAB[a]], writes=[rAB[a]])
            for (dr, ob, h0, nh, a) in items:
                W = nh * 128
                s.op("act", "activation", dict(out=AB[a][0:64, 0:W], in_=AB[a][0:64, 0:W], func=AF.Ln), reads=[rAB[a]], writes=[rAB[a]])
            for (dr, ob, h0, nh, a) in items:
                W = nh * 128
                s.op("act", "activation", dict(out=AB[a][0:64, 0:W], in_=AB[a][0:64, 0:W], func=AF.Exp, scale=-1.0), reads=[rAB[a]], writes=[rAB[a]])
            for (dr, ob, h0, nh, a) in items:
                W = nh * 128
                hsb, rhs_ = HS[dr], rHS[dr]
                s.op("dve", "tensor_tensor", dict(out=hsb[0:64, h0:h0 + nh, :], in0=PSB[ob][0:64, 0:W].rearrange("p (h t) -> p h t", h=nh),
                                                  in1=AB[a][0:64, 0:W].rearrange("p (h t) -> p h t", h=nh), op=ALU.mult),
                     reads=[rPS[ob], rAB[a]], parts=[rhs_])
            for dr in range(2):
                j = tile_of(dr, g)
                dma(dict(out=HD.ap()[dr, :, j * 128:(j + 1) * 128].rearrange("(h e) t -> e h t", e=64), in_=HS[dr][0:64, :, :]),
                    reads=[rHS[dr]], writes=[res("HD", dr, j)])

    def phase_mlstm_post(l):
        phase()
        NBF = 2
        HF = [ar.alloc([128, 3, 512], F32, "HF") for _ in range(NBF)]; rHF = [Res() for _ in range(NBF)]
        HB = [ar.alloc([128, 3, 512], F32, "HB") for _ in range(NBF)]; rHB = [Res() for _ in range(NBF)]
        OC = [ar.alloc([128, 3, 512], BF16, "OC") for _ in range(NBF)]; rOC = [Res() for _ in range(NBF)]
        SQ = ar.alloc([128, 3, 512], BF16, "SQm"); rSQ = Res()
        RS = ar.alloc([128, 3, 512], F32, "RSm"); rRS = Res()
        SG = ar.alloc([128, 3, 512], F32, "SGm"); rSG = Res()
        YS = [ar.alloc([128, 3, 512], BF16, "YSm") for _ in range(2)]; rYS = [Res(), Res()]
        hd_all = [res("HD", dr, j) for dr in range(2) for j in range(NT128)]
        zt_all = [res("ZT", t) for t in range(NT512)]

        def ld(t):
            sl = t % NBF
            dma(dict(out=HF[sl][:], in_=HD.ap()[0, :, t * 512:(t + 1) * 512].rearrange("(b p) t -> p b t", p=128)), reads=hd_all, writes=[rHF[sl]])
            dma(dict(out=HB[sl][:], in_=HD.ap()[1, :, t * 512:(t + 1) * 512].rearrange("(b p) t -> p b t", p=128)), reads=hd_all, writes=[rHB[sl]])
            dma(dict(out=OC[sl][:], in_=ZT.ap()[15 * 128:18 * 128, PADK + t * 512:PADK + (t + 1) * 512].rearrange("(b p) t -> p b t", p=128)),
                reads=zt_all, writes=[rOC[sl]])
        ld(0)
        for t in range(NT512):
            if t + 1 < NT512:
                ld(t + 1)
            sl = t % NBF
            ys, rys = YS[t % 2], rYS[t % 2]
            s.op("dve", "tensor_tensor", dict(out=HF[sl][:], in0=HF[sl][:], in1=HB[sl][:], op=ALU.add), reads=[rHF[sl], rHB[sl]], writes=[rHF[sl]])
            s.op("act", "activation", dict(out=SQ[:], in_=HF[sl][:], func=AF.Square), reads=[rHF[sl]], writes=[rSQ])
            for b in range(3):
                s.op("pe", "matmul", dict(out=PSB[b][:, :], lhsT=bd_bf[:], rhs=SQ[:, b, :], start=True, stop=True), reads=[rSQ, r_const], writes=[rPS[b]])
            for b in range(3):
                s.op("act", "activation", dict(out=RS[:, b, :], in_=PSB[b][:, :], func=AF.Ln, scale=1.0 / 64, bias=EPS), reads=[rPS[b]], parts=[rRS])
            s.op("act", "activation", dict(out=RS[:], in_=RS[:], func=AF.Exp, scale=-0.5), reads=[rRS], writes=[rRS])
            s.op("act", "activation", dict(out=SG[:], in_=OC[sl][:], func=AF.Sigmoid), reads=[rOC[sl]], writes=[rSG])
            for b in range(3):
                s.op("dve", "scalar_tensor_tensor", dict(out=RS[:, b, :], in0=HF[sl][:, b, :], scalar=gM[:, l, b:b + 1], in1=RS[:, b, :], op0=ALU.mult, op1=ALU.mult), reads=[rHF[sl], rRS, r_const], writes=[rRS])
            s.op("pool", "tensor_tensor", dict(out=ys[:], in0=RS[:], in1=SG[:], op=ALU.mult), reads=[rRS, rSG], writes=[rys])
            dma(dict(out=YT.ap()[640:1024, t * 512:(t + 1) * 512].rearrange("(b p) t -> p b t", p=128), in_=ys[:]),
                reads=[rys], writes=[res("YT", "C", t)])

    def phase_outproj(l):
        phase()
        Wo = ar.alloc([128, 8, D], BF16, "Wo"); rWo = Res()
        load_weight(Wo, rWo, wout_in.ap()[l], D, D)
        NBF = 2
        XS = [ar.alloc([128, 8, 512], F32, "XSo") for _ in range(NBF)]; rXS = [Res() for _ in range(NBF)]
        YS = [ar.alloc([128, 8, 512], BF16, "YSo") for _ in range(NBF)]; rYS = [Res() for _ in range(NBF)]
        yt_all = ([res("YT", "A", h, sb) for h in range(6) for sb in range(NSB)] + [res("YT", "B", h, g) for h in range(4) for g in range(NT512)]
                  + [res("YT", "C", t) for t in range(NT512)])

        def ld(t):
            sl = t % NBF
            dma(dict(out=XS[sl][:], in_=XT.ap()[:, t * 512:(t + 1) * 512].rearrange("(c p) t -> p c t", p=128)), reads=[res("XT", t)], writes=[rXS[sl]])
            dma(dict(out=YS[sl][:], in_=YT.ap()[:, t * 512:(t + 1) * 512].rearrange("(c p) t -> p c t", p=128)), reads=yt_all, writes=[rYS[sl]])
        ld(0)
        for t in range(NT512):
            if t + 1 < NT512:
                ld(t + 1)
            sl = t % NBF
            for m in range(8):
                bk = m % 4
                for k in range(8):
                    s.op("pe", "matmul", dict(out=PSB[bk][:, :], lhsT=Wo[:, k, m * 128:(m + 1) * 128], rhs=YS[sl][:, k, :], start=(k == 0), stop=(k == 7)),
                         reads=[rWo, rYS[sl]], writes=[rPS[bk]])
                s.op("dve", "tensor_tensor", dict(out=XS[sl][:, m, :], in0=PSB[bk][:, :], in1=XS[sl][:, m, :], op=ALU.add),
                     reads=[rPS[bk], rXS[sl]], writes=[rXS[sl]])
            dma(dict(out=XT.ap()[:, t * 512:(t + 1) * 512].rearrange("(c p) t -> p c t", p=128), in_=XS[sl][:]),
                reads=[rXS[sl]], writes=[res("XT", t)])

    def phase_ffn(l):
        phase()
        NJ = DFF // 128
        Wu = ar.alloc([128, 8, 2 * DFF], BF16, "Wu"); rWu = Res()
        Wd = ar.alloc([128, NJ, D], BF16, "Wd"); rWd = Res()
        mark = ar.off
        load_weight(Wu, rWu, wup_in.ap()[l], D, 2 * DFF)
        load_weight(Wd, rWd, wdn_in.ap()[l], DFF, D)
        ar.off = mark
        s.barrier()
        T = 256
        NBF = 2
        XS = [ar.alloc([128, 8, T], F32, "XSf") for _ in range(NBF)]; rXS = [Res() for _ in range(NBF)]
        HT = ar.alloc([128, 8, T], BF16, "HTf"); rHT = Res()
        SQ = ar.alloc([128, 8, T], BF16, "SQf"); rSQ = Res()
        RS = ar.alloc([128, T], F32, "RSf"); rRS = Res()
        AT = ar.alloc([128, NJ, T], BF16, "ATf"); rAT = Res()
        SGs = [ar.alloc([128, T], F32, "SGf") for _ in range(3)]; rSG = [Res() for _ in range(3)]

        def ld(t):
            sl = t % NBF
            dma(dict(out=XS[sl][:], in_=XT.ap()[:, t * T:(t + 1) * T].rearrange("(c p) t -> p c t", p=128)),
                reads=[res("XT", t // 2)], writes=[rXS[sl]])
        ld(0)
        for t in range(NT256):
            if t + 1 < NT256:
                ld(t + 1)
            sl = t % NBF
            rmsnorm_tile(XS[sl], rXS[sl], HT, rHT, lambda c: gF[:, l, c:c + 1], T, SQ, rSQ, RS, rRS, 0)
            for jb in range(NJ):
                bk = 1 + jb % 3
                for k in range(8):
                    s.op("pe", "matmul", dict(out=PSB[bk][:, 0:T], lhsT=Wu[:, k, jb * 128:(jb + 1) * 128], rhs=HT[:, k, :], start=(k == 0), stop=(k == 7)),
                         reads=[rWu, rHT], writes=[rPS[bk]])
                for k in range(8):
                    s.op("pe", "matmul", dict(out=PSB[bk][:, T:2 * T], lhsT=Wu[:, k, DFF + jb * 128:DFF + (jb + 1) * 128], rhs=HT[:, k, :], start=(k == 0), stop=(k == 7)),
                         reads=[rWu, rHT], writes=[rPS[bk]])
                sg = jb % 3
                s.op("act", "activation", dict(out=SGs[sg][:], in_=PSB[bk][:, 0:T], func=AF.Silu), reads=[rPS[bk]], writes=[rSG[sg]])
                s.op("dve", "tensor_tensor", dict(out=AT[:, jb, :], in0=PSB[bk][:, T:2 * T], in1=SGs[sg][:], op=ALU.mult),
                     reads=[rPS[bk], rSG[sg]], parts=[rAT])
            for m in range(8):
                bk = 4 + m % 4
                for jb in range(NJ):
                    s.op("pe", "matmul", dict(out=PSB[bk][:, 0:T], lhsT=Wd[:, jb, m * 128:(m + 1) * 128], rhs=AT[:, jb, :], start=(jb == 0), stop=(jb == NJ - 1)),
                         reads=[rWd, rAT], writes=[rPS[bk]])
                s.op("dve", "tensor_tensor", dict(out=XS[sl][:, m, :], in0=PSB[bk][:, 0:T], in1=XS[sl][:, m, :], op=ALU.add),
                     reads=[rPS[bk], rXS[sl]], writes=[rXS[sl]])
            dma(dict(out=XT.ap()[:, t * T:(t + 1) * T].rearrange("(c p) t -> p c t", p=128), in_=XS[sl][:]),
                reads=[rXS[sl]], writes=[res("XT2", t)])

    def phase_ple(l):
        phase()
        last = (l == depth - 1)
        Wg = ar.alloc([128, 8, D], BF16, "Wg"); rWg = Res()
        Wp = ar.alloc([128, 2, D], BF16, "Wp"); rWp = Res()
        load_weight(Wg, rWg, wpg_in.ap()[l], D, D)
        load_weight(Wp, rWp, wpp_in.ap()[l], PLE, D)
        if not last:
            Win, rWin = load_win(l + 1)
            bufs = inproj_bufs()
            HT, rHT, SQ, rSQ, RS, rRS = bufs[0:6]
        else:
            HT = ar.alloc([128, 8, 512], F32, "HTo"); rHT = Res()
            SQ = ar.alloc([128, 8, 512], BF16, "SQo"); rSQ = Res()
            RS = ar.alloc([128, 512], F32, "RSo"); rRS = Res()
        NBF = 2
        XS = [ar.alloc([128, 8, 512], F32, "XSp") for _ in range(NBF)]; rXS = [Res() for _ in range(NBF)]
        PS_ = [ar.alloc([128, 2, 512], F32, "PSp") for _ in range(NBF)]; rPS_ = [Res() for _ in range(NBF)]
        XB, rXB = SQ, rSQ
        PB = ar.alloc([128, 2, 512], BF16, "PBp"); rPB = Res()
        SGs = [ar.alloc([128, 512], F32, "SGp") for _ in range(2)]; rSG = [Res(), Res()]
        fin = []

        def ld(t):
            sl = t % NBF
            dma(dict(out=XS[sl][:], in_=XT.ap()[:, t * 512:(t + 1) * 512].rearrange("(c p) t -> p c t", p=128)),
                reads=[res("XT2", 2 * t), res("XT2", 2 * t + 1)], writes=[rXS[sl]])
            dma(dict(out=PS_[sl][:], in_=pT_in.ap()[l, :, t * 512:(t + 1) * 512].rearrange("(c p) t -> p c t", p=128)), writes=[rPS_[sl]])
        def ple_part(t):
            sl = t % NBF
            s.op("pool", "tensor_copy", dict(out=XB[:], in_=XS[sl][:]), reads=[rXS[sl]], writes=[rXB])
            s.op("act", "activation", dict(out=PB[:], in_=PS_[sl][:], func=AF.Copy), reads=[rPS_[sl]], writes=[rPB])
            for m in range(8):
                bk = 4 + (m % 2) * 2
                for k in range(8):
                    s.op("pe", "matmul", dict(out=PSB[bk][:, :], lhsT=Wg[:, k, m * 128:(m + 1) * 128], rhs=XB[:, k, :], start=(k == 0), stop=(k == 7)),
                         reads=[rWg, rXB], writes=[rPS[bk]])
                for k in range(2):
                    s.op("pe", "matmul", dict(out=PSB[bk + 1][:, :], lhsT=Wp[:, k, m * 128:(m + 1) * 128], rhs=PB[:, k, :], start=(k == 0), stop=(k == 1)),
                         reads=[rWp, rPB], writes=[rPS[bk + 1]])
                sg = m % 2
                s.op("act", "activation", dict(out=SGs[sg][:], in_=PSB[bk][:, :], func=AF.Sigmoid), reads=[rPS[bk]], writes=[rSG[sg]])
                s.op("dve", "tensor_tensor", dict(out=SGs[sg][:], in0=PSB[bk + 1][:, :], in1=SGs[sg][:], op=ALU.mult),
                     reads=[rPS[bk + 1], rSG[sg]], writes=[rSG[sg]])
                s.op("pool", "tensor_tensor", dict(out=XS[sl][:, m, :], in0=SGs[sg][:], in1=XS[sl][:, m, :], op=ALU.add),
                     reads=[rSG[sg], rXS[sl]], writes=[rXS[sl]])
            if not last:
                dma(dict(out=XT.ap()[:, t * 512:(t + 1) * 512].rearrange("(c p) t -> p c t", p=128), in_=XS[sl][:]),
                    reads=[rXS[sl]], writes=[res("XT", t)])

        def norm_part(t):
            sl = t % NBF
            if not last:
                inproj_tile(t, XS[sl], rXS[sl], Win, rWin, l + 1, bufs, part="norm")
            else:
                s.op("act", "activation", dict(out=SQ[:], in_=XS[sl][:], func=AF.Square), reads=[rXS[sl]], writes=[rSQ])
                for c in range(8):
                    s.op("pe", "matmul", dict(out=PSB[0][:, :], lhsT=ones_bf[:], rhs=SQ[:, c, :], start=(c == 0), stop=(c == 7)), reads=[rSQ, r_const], writes=[rPS[0]])
                s.op("act", "activation", dict(out=RS[:], in_=PSB[0][:, :], func=AF.Ln, scale=1.0 / D, bias=EPS), reads=[rPS[0]], writes=[rRS])
                s.op("act", "activation", dict(out=RS[:], in_=RS[:], func=AF.Exp, scale=-0.5), reads=[rRS], writes=[rRS])
                for c in range(8):
                    s.op("dve", "scalar_tensor_tensor", dict(out=HT[:, c, :], in0=XS[sl][:, c, :], scalar=gfin[:, c:c + 1], in1=RS[:], op0=ALU.mult, op1=ALU.mult),
                         reads=[rXS[sl], rRS, r_const], parts=[rHT])
                fin.append(dma(dict(out=out_d.ap()[:, t * 512:(t + 1) * 512].rearrange("(c p) t -> p c t", p=128), in_=HT[:]),
                               reads=[rHT], writes=[res("OUT", t)]))

        ld(0)
        if NT512 > 1:
            ld(1)
        ple_part(0)
        for t in range(NT512):
            norm_part(t)
            if t + 2 < NT512:
                ld(t + 2)
            if t + 1 < NT512:
                ple_part(t + 1)
            if not last:
                inproj_tile(t, XS[t % NBF], rXS[t % NBF], Win, rWin, l + 1, bufs, part="proj")
        return fin

    plist = [("init", phase_init, ()), ("inproj0", phase_inproj0, ())]
    for l in range(depth):
        plist += [("gates", phase_gates, (l,)), ("attnA", phase_attnA, (l,)), ("attnB", phase_attnB, (l,)),
                  ("mpre", phase_mlstm_pre, (l,)), ("mrec", phase_mlstm_rec, (l,)), ("mpost", phase_mlstm_post, (l,)),
                  ("outproj", phase_outproj, (l,)), ("ffn", phase_ffn, (l,)), ("ple", phase_ple, (l,))]
    for (name, fn, args) in plist:
        fn(*args)
        if upto is not None and name == upto:
            break
    s.emit(final_tokens=s.all_tokens())
    return nc, s


_COLS = dict(qa=0, ka=384, va=768, qb=1152, kb=1408, vb=1536, qc=1664, kc=2048, vc=2432, oc=2816, gc=3200)


def _selectors():
    sel = np.zeros((36, 18, 128), np.float32)
    for dr in range(2):
        for qty in range(3):
            for b in range(3):
                idx = dr * 9 + qty * 3 + b
                for hh in range(2):
                    sel[qty * 12 + dr * 6 + 2 * b + hh, idx, hh * 64:(hh + 1) * 64] = 1.0
    return sel


def _win_perm():
    c = _COLS
    r = lambda a, n: list(range(a, a + n))
    qb = c["qb"]
    perm = (r(c["qa"], 384) + r(c["ka"], 384) + r(qb, 64) + r(qb + 128, 64) + r(qb + 64, 64) + r(qb + 192, 64)
            + r(c["kb"], 128) + r(c["qc"], 384) + r(c["kc"], 384) + r(c["oc"], 384) + r(c["gc"], 24)
            + r(c["va"], 384) + r(c["vb"], 128) + r(c["vc"], 384))
    assert len(perm) == DIN and len(set(perm)) == DIN
    return np.array(perm)


def host_inputs(inputs, S, depth, b):
    f = lambda a: np.ascontiguousarray(np.asarray(a, dtype=np.float32))
    x = np.asarray(inputs["x"]); p = np.asarray(inputs["p"])
    bc = lambda a: np.ascontiguousarray(np.broadcast_to(np.asarray(a, np.float32)[None], (128,) + tuple(np.asarray(a).shape)))
    i = np.arange(128)
    m = {
        "xT": f(x[b].T),
        "pT": f(np.transpose(p[:depth, b], (0, 2, 1))),
        "relb": f(inputs["rel_bias"]),
        "gA": f(np.asarray(inputs["attn_norm"])[:depth].reshape(depth, 8, 128).transpose(2, 0, 1)),
        "gF": f(np.asarray(inputs["ffn_norm"])[:depth].reshape(depth, 8, 128).transpose(2, 0, 1)),
        "gfin": f(np.asarray(inputs["final_norm"]).reshape(8, 128).T),
        "gM": f(np.asarray(inputs["mlstm_norm"])[:depth].reshape(depth, 3, 128).transpose(2, 0, 1)),
        "cw": f(np.asarray(inputs["qk_conv"])[:depth].reshape(depth, 5, 6, 128).transpose(3, 0, 2, 1)),
        "gb": bc(np.asarray(inputs["gate_bias"])[:depth]),
        "sk": bc(np.asarray(inputs["sink_logits"])[:depth]),
        "w_in": f(np.asarray(inputs["w_in"])[:depth][:, :, _win_perm()]),
        "w_out": f(np.asarray(inputs["w_out"])[:depth]),
        "w_up": f(np.asarray(inputs["w_up"])[:depth]),
        "w_down": f(np.asarray(inputs["w_down"])[:depth]),
        "ple_proj": f(np.asarray(inputs["ple_proj"])[:depth]),
        "ple_gate": f(np.asarray(inputs["ple_gate"])[:depth]),
        "oh": _onehots(),
        "ident": np.eye(128, dtype=np.float32),
        "maskF": (i[:, None] <= i[None, :]).astype(np.float32),
        "maskB": (i[:, None] >= i[None, :]).astype(np.float32),
        "sel": _selectors(),
    }
    return m


_CACHE = {}


def kernel(**inputs):
    x = np.asarray(inputs["x"])
    B, S, _ = x.shape
    depth = np.asarray(inputs["w_in"]).shape[0]
    key = (S, depth)
    if key not in _CACHE:
        _CACHE[key] = build(S, depth)[0]
    nc = _CACHE[key]
    shared = None
    in_maps = []
    for b in range(B):
        m = host_inputs(inputs, S, depth, b) if shared is None else dict(shared)
        if shared is None:
            shared = {k: v for k, v in m.items() if k not in ("xT", "pT")}
        else:
            m["xT"] = np.ascontiguousarray(x[b].T.astype(np.float32))
            m["pT"] = np.ascontiguousarray(np.transpose(np.asarray(inputs["p"])[:depth, b], (0, 2, 1)).astype(np.float32))
        in_maps.append(m)
    res = run_bass_kernel_spmd(nc, in_maps, core_ids=list(range(B)))
    out = np.stack([np.ascontiguousarray(r["outT"].T) for r in res.results], axis=0)
    return out.astype(np.float32)
```

```python
import math
import os
import numpy as np
import concourse.bass as bass
import concourse.mybir as mybir
from concourse.bass_utils import run_bass_kernel_spmd

F32 = mybir.dt.float32
BF16 = mybir.dt.bfloat16
ALU = mybir.AluOpType
AF = mybir.ActivationFunctionType

D = 1024
DEPTH = 4
NHA, NHB, NKVB, NHC = 6, 4, 2, 6
DFF = 2816
PLE = 256
DIN = 3224
EPS = 1e-6
PADK = 1024
NTAB = 512
NEGB = -30000.0

ENGS = ("pe", "act", "dve", "pool", "sp")
CH = 30000
NDS = 48


class Res:
    __slots__ = ("writers", "readers", "prev")

    def __init__(self):
        self.writers = []
        self.readers = []
        self.prev = []


class Sched:
    def __init__(self, nc):
        self.nc = nc
        self.ops = {e: [] for e in ENGS}
        self.ndma = 0
        self.pending = {e: [] for e in ENGS}
        self.live_dma = []

    def _deps(self, eng, reads, writes, parts):
        deps = []
        for r in reads:
            deps.extend(r.writers)
        for w in writes:
            deps.extend(w.readers)
            deps.extend(w.writers)
            deps.extend(w.prev)
        for w in parts:
            if w.readers:
                w.prev = w.writers + w.readers
                w.writers = []
                w.readers = []
            deps.extend(w.prev)
        if self.pending[eng]:
            deps.extend(self.pending[eng])
            self.pending[eng] = []
        return deps

    def _commit(self, tok, reads, writes, parts):
        for r in reads:
            r.readers.append(tok)
        for w in writes:
            w.writers = [tok]
            w.readers = []
            w.prev = [tok]
        for w in parts:
            w.writers.append(tok)

    def op(self, eng, meth, kw, reads=(), writes=(), parts=()):
        deps = self._deps(eng, reads, writes, parts)
        self.ops[eng].append({"fn": (meth, kw), "deps": deps, "sig": False, "dma": None})
        tok = ("e", eng, len(self.ops[eng]) - 1)
        self._commit(tok, reads, writes, parts)
        return tok

    def dma(self, eng, kw, reads=(), writes=(), parts=()):
        deps = self._deps(eng, reads, writes, parts)
        i = self.ndma
        self.ndma += 1
        if i >= NDS:
            deps.append(("d", i - NDS))
        self.ops[eng].append({"fn": ("dma_start", kw), "deps": deps, "sig": False, "dma": i})
        tok = ("d", i)
        self.live_dma.append(tok)
        if len(self.live_dma) > NDS:
            self.live_dma = self.live_dma[-NDS:]
        self._commit(tok, reads, writes, parts)
        return tok

    def all_tokens(self):
        toks = []
        for e in ENGS:
            for k in range(len(self.ops[e]) - 1, -1, -1):
                if self.ops[e][k]["dma"] is None:
                    toks.append(("e", e, k))
                    break
        toks.extend(self.live_dma)
        return toks

    def barrier(self):
        toks = []
        for e in ENGS:
            if self.ops[e]:
                for k in range(len(self.ops[e]) - 1, -1, -1):
                    if self.ops[e][k]["dma"] is None:
                        toks.append(("e", e, k))
                        break
        toks.extend(self.live_dma)
        for e in ENGS:
            self.pending[e] = list(self.pending[e]) + toks

    def emit(self, final_tokens=()):
        nc = self.nc
        for e in ENGS:
            for rec in self.ops[e]:
                for t in rec["deps"]:
                    if t[0] == "e":
                        self.ops[t[1]][t[2]]["sig"] = True
        for t in final_tokens:
            if t[0] == "e":
                self.ops[t[1]][t[2]]["sig"] = True
        nsig = {}
        for e in ENGS:
            c = 0
            for rec in self.ops[e]:
                if rec["sig"]:
                    c += 1
                    rec["cnt"] = c
            nsig[e] = c
        esems = {e: [nc.alloc_semaphore(f"s_{e}_{j}") for j in range(max(1, (nsig[e] + CH - 1) // CH))]
                 for e in ENGS}
        dsems = [nc.alloc_semaphore(f"s_dma_{j}") for j in range(min(NDS, max(1, self.ndma)))]
        ops = self.ops

        def tok_wait(t):
            if t[0] == "e":
                c = ops[t[1]][t[2]]["cnt"]
                return (("e", t[1], (c - 1) // CH), esems[t[1]][(c - 1) // CH], (c - 1) % CH + 1)
            i = t[1]
            return (("d", i % NDS), dsems[i % NDS], 16 * (i // NDS + 1))

        self.nwaits = 0
        with nc.Block() as block:
            def run(ename):
                def body(eng):
                    known = {}
                    for rec in ops[ename]:
                        need = {}
                        for t in rec["deps"]:
                            if t[0] == "e" and t[1] == ename and ename == "pe":
                                continue
                            key, sem, val = tok_wait(t)
                            if known.get(key, 0) >= val:
                                continue
                            if key not in need or need[key][1] < val:
                                need[key] = (sem, val)
                        for key, (sem, val) in need.items():
                            eng.wait_ge(sem, val)
                            known[key] = val
                            self.nwaits += 1
                        meth, kw = rec["fn"]
                        ins = getattr(eng, meth)(**kw)
                        if rec["dma"] is not None:
                            ins.then_inc(dsems[rec["dma"] % NDS], 16)
                        elif rec["sig"]:
                            c = rec["cnt"]
                            ins.then_inc(esems[ename][(c - 1) // CH], 1)
                    if ename == "sp":
                        for t in final_tokens:
                            key, sem, val = tok_wait(t)
                            if known.get(key, 0) < val:
                                eng.wait_ge(sem, val)
                                known[key] = val
                return body
            block.tensor(run("pe"))
            block.scalar(run("act"))
            block.vector(run("dve"))
            block.gpsimd(run("pool"))
            block.sync(run("sp"))


def _dsize(dt):
    return 2 if dt == BF16 else 4


class Arena:
    def __init__(self, nc, limit=229376):
        self.nc = nc
        self.off = 16640
        self.limit = limit
        self.n = 0

    def alloc(self, shape, dtype, name="t"):
        nbytes = int(np.prod(shape[1:])) * _dsize(dtype)
        nbytes = (nbytes + 63) // 64 * 64
        assert self.off + nbytes <= self.limit, (name, self.off, nbytes)
        self.n += 1
        t = self.nc.alloc_sbuf_tensor_at(f"{name}_{self.n}", list(shape), dtype, offset=self.off)
        self.off += nbytes
        return t


def _t5_bucket(rel):
    rel = np.asarray(rel, dtype=np.int64)
    n = np.abs(rel)
    nf = np.maximum(n, 1).astype(np.float32)
    large = 8 + (np.log(nf / np.float32(8)) / np.float32(math.log(1024 / 8)) * np.float32(8)).astype(np.int32)
    large = np.minimum(large, 15)
    return np.where(rel > 0, 16, 0) + np.where(n < 8, n, large)


def _onehots():
    oh = np.zeros((4, 33, NTAB), np.float32)
    m = np.arange(NTAB)
    rel = np.where(m <= NTAB // 2, -m, NTAB - m)
    for ti, (dil, band) in enumerate(((1, 64), (4, 64), (16, 64), (1, 128))):
        b = _t5_bucket(rel * dil)
        inb = np.abs(rel) <= band
        for j in range(NTAB):
            if inb[j]:
                oh[ti, b[j], j] = 1.0
            else:
                oh[ti, 32, j] = 1.0
    return oh


def build(S=8192, depth=DEPTH, dbg=(), upto=None):
    nc = bass.Bass("TRN2", target_bir_lowering=False)
    s = Sched(nc)
    NT512 = S // 512
    NT256 = S // 256
    NT128 = S // 128
    NSB = S // 2048
    SP = S + 2 * PADK

    def dram_in(name, shape, dt=F32):
        return nc.dram_tensor(name, list(shape), dt, kind="ExternalInput")

    def dram_scr(name, shape, dt):
        return nc.dram_tensor(name, list(shape), dt, kind=("ExternalOutput" if name in dbg else "Internal"))

    xT_in = dram_in("xT", [D, S])
    pT_in = dram_in("pT", [depth, PLE, S])
    relb_in = dram_in("relb", [32, 10])
    gA_in = dram_in("gA", [128, depth, 8])
    gF_in = dram_in("gF", [128, depth, 8])
    gfin_in = dram_in("gfin", [128, 8])
    gM_in = dram_in("gM", [128, depth, 3])
    cw_in = dram_in("cw", [128, depth, 6, 5])
    gb_in = dram_in("gb", [128, depth, 24])
    sk_in = dram_in("sk", [128, depth, 4])
    win_in = dram_in("w_in", [depth, D, DIN])
    wout_in = dram_in("w_out", [depth, D, D])
    wup_in = dram_in("w_up", [depth, D, 2 * DFF])
    wdn_in = dram_in("w_down", [depth, DFF, D])
    wpp_in = dram_in("ple_proj", [depth, PLE, D])
    wpg_in = dram_in("ple_gate", [depth, D, D])
    oh_in = dram_in("oh", [4, 33, NTAB])
    ident_in = dram_in("ident", [128, 128])
    mF_in = dram_in("maskF", [128, 128])
    mB_in = dram_in("maskB", [128, 128])
    sel_in = dram_in("sel", [36, 18, 128])
    out_d = nc.dram_tensor("outT", [D, S], F32, kind="ExternalOutput")

    XT = dram_scr("XT", [D, S], F32)
    ZT = dram_scr("ZT", [2304, SP], BF16)
    GT = dram_scr("GT", [24, S], F32)
    VT = dram_scr("VT", [SP, 14, 128], BF16)
    YT = dram_scr("YT", [D, S], BF16)
    MT = dram_scr("MT", [18 * 128, S], BF16)
    EG = dram_scr("EG", [36, S], F32)
    DECD = dram_scr("DECD", [2, NT128, 6], F32)
    MTK = dram_scr("MTK", [2, S, 3, 128], BF16)
    HD = dram_scr("HD", [2, 384, S], F32)
    TP = dram_scr("TP", [22, 128, NTAB], BF16)

    R = {}

    def res(*key):
        if key not in R:
            R[key] = Res()
        return R[key]

    ar = Arena(nc)
    ones_bf = ar.alloc([128, 128], BF16, "ones")
    bd_bf = ar.alloc([128, 128], BF16, "bd")
    ident_bf = ar.alloc([128, 128], BF16, "ident")
    maskF = ar.alloc([128, 128], F32, "maskF")
    maskB = ar.alloc([128, 128], F32, "maskB")
    EMA = ar.alloc([128, 3, 6, 256], BF16, "EMA")
    EMB = ar.alloc([128, 4, 384], BF16, "EMB")
    gA = ar.alloc([128, depth, 8], F32, "gA")
    gF = ar.alloc([128, depth, 8], F32, "gF")
    gfin = ar.alloc([128, 8], F32, "gfin")
    gM = ar.alloc([128, depth, 3], F32, "gM")
    cw = ar.alloc([128, depth, 6, 5], F32, "cw")
    gb = ar.alloc([128, depth, 24], F32, "gb")
    ES = ar.alloc([128, depth, 4], F32, "ES")
    zeros_bf = ar.alloc([128, 1792], BF16, "zeros")
    r_const = Res()
    PERSIST = ar.off

    PSB = [nc.alloc_psum_tensor(f"psb{i}", [128, 512], F32) for i in range(8)]
    rPS = [Res() for _ in range(8)]
    STG = {}

    def phase():
        s.barrier()
        ar.off = PERSIST
        STG.clear()

    def dma(kw, reads=(), writes=(), parts=(), q="sp"):
        return s.dma(q, kw, reads=reads, writes=writes, parts=parts)

    cast_rr = [0]

    def cast_copy(out_ap, in_ap, reads, writes=(), parts=(), engs=("act", "dve")):
        e = engs[cast_rr[0] % len(engs)]
        cast_rr[0] += 1
        if e == "act":
            s.op("act", "activation", dict(out=out_ap, in_=in_ap, func=AF.Copy), reads=reads, writes=writes, parts=parts)
        else:
            s.op(e, "tensor_copy", dict(out=out_ap, in_=in_ap), reads=reads, writes=writes, parts=parts)

    def load_weight(dst, rdst, src2d, K, N):
        CW = 1024
        if "b" not in STG:
            STG["b"] = [ar.alloc([128, CW], F32, "wstg") for _ in range(4)]
            STG["r"] = [Res() for _ in range(4)]
            STG["i"] = 0
        for k in range(K // 128):
            for c0 in range(0, N, CW):
                n = min(CW, N - c0)
                sl = STG["i"] % 4
                STG["i"] += 1
                st, rs = STG["b"][sl], STG["r"][sl]
                dma(dict(out=st[:, 0:n], in_=src2d[k * 128:(k + 1) * 128, c0:c0 + n]), writes=[rs])
                cast_copy(dst[:, k, c0:c0 + n], st[:, 0:n], reads=[rs], parts=[rdst])

    def rmsnorm_tile(XS, rXS, HT, rHT, gtile_ap_fn, T, SQ, rSQ, RS, rRS, bank):
        s.op("act", "activation", dict(out=SQ[:, :, 0:T], in_=XS[:, :, 0:T], func=AF.Square), reads=[rXS], writes=[rSQ])
        for c in range(8):
            s.op("pe", "matmul", dict(out=PSB[bank][:, 0:T], lhsT=ones_bf[:], rhs=SQ[:, c, 0:T], start=(c == 0), stop=(c == 7)),
                 reads=[rSQ, r_const], writes=[rPS[bank]])
        s.op("act", "activation", dict(out=RS[:, 0:T], in_=PSB[bank][:, 0:T], func=AF.Ln, scale=1.0 / D, bias=EPS),
             reads=[rPS[bank]], writes=[rRS])
        s.op("act", "activation", dict(out=RS[:, 0:T], in_=RS[:, 0:T], func=AF.Exp, scale=-0.5), reads=[rRS], writes=[rRS])
        for c in range(8):
            s.op("dve", "scalar_tensor_tensor", dict(out=HT[:, c, 0:T], in0=XS[:, c, 0:T], scalar=gtile_ap_fn(c), in1=RS[:, 0:T], op0=ALU.mult, op1=ALU.mult),
                 reads=[rXS, rRS, r_const], parts=[rHT])

    def phase_init():
        stg = ar.alloc([128, 128], F32, "istg")
        rstg = Res()
        s.op("dve", "memset", dict(ap=ones_bf[:], constant=1.0), writes=[r_const])
        r_bd = Res()
        s.op("dve", "memset", dict(ap=bd_bf[:], constant=0.0), writes=[r_bd])
        s.op("dve", "memset", dict(ap=bd_bf[0:64, 0:64], constant=1.0), writes=[r_bd])
        s.op("dve", "memset", dict(ap=bd_bf[64:128, 64:128], constant=1.0), writes=[r_bd])
        s.op("dve", "memset", dict(ap=zeros_bf[:], constant=0.0), parts=[r_const])
        dma(dict(out=stg[:], in_=ident_in.ap()), writes=[rstg])
        s.op("dve", "tensor_copy", dict(out=ident_bf[:], in_=stg[:]), reads=[rstg], parts=[r_const])
        for (t, src) in ((maskF, mF_in), (maskB, mB_in), (gA, gA_in), (gF, gF_in), (gfin, gfin_in), (gM, gM_in),
                         (cw, cw_in), (gb, gb_in), (ES, sk_in)):
            dma(dict(out=t[:], in_=src.ap()), parts=[r_const])
        s.op("act", "activation", dict(out=ES[:], in_=ES[:], func=AF.Exp), reads=[r_const], parts=[r_const])
        for b in range(18):
            for c0 in (0, PADK + S):
                dma(dict(out=ZT.ap()[b * 128:(b + 1) * 128, c0:c0 + PADK], in_=zeros_bf[:, 0:PADK]),
                    reads=[r_const], parts=[res("ZTpad")])
        for r0 in list(range(0, PADK, 128)) + list(range(PADK + S, SP, 128)):
            dma(dict(out=VT.ap()[r0:r0 + 128].rearrange("p h e -> p (h e)"), in_=zeros_bf[:, 0:1792]),
                reads=[r_const], parts=[res("VTpad")])
        BX = ar.alloc([128, 10], F32, "BX")
        OH = ar.alloc([128, 4, NTAB], F32, "OH")
        rBX, rOH = Res(), Res()
        s.op("dve", "memset", dict(ap=BX[32:64, :], constant=NEGB), writes=[rBX])
        dma(dict(out=BX[0:32, :], in_=relb_in.ap()), parts=[rBX])
        dma(dict(out=OH[0:33, :, :], in_=oh_in.ap().rearrange("t b n -> b t n")), writes=[rOH])
        TS = [ar.alloc([128, NTAB], BF16, "TS") for _ in range(2)]
        rTS = [Res(), Res()]
        tabs = [(pi, h, h, pi * 6 + h) for pi in range(3) for h in range(6)] + [(3, h, 6 + h, 18 + h) for h in range(4)]
        for i, (ti, h, col, tab) in enumerate(tabs):
            bk = i % 2
            s.op("pe", "matmul", dict(out=PSB[bk][:, :], lhsT=BX[0:33, col:col + 1].to_broadcast([33, 128]), rhs=OH[0:33, ti, :], start=True, stop=True),
                 reads=[rBX, rOH], writes=[rPS[bk]])
            s.op("act", "activation", dict(out=TS[bk][:], in_=PSB[bk][:, :], func=AF.Copy, scale=8.0), reads=[rPS[bk]], writes=[rTS[bk]])
            dma(dict(out=TP.ap()[tab], in_=TS[bk][:]), reads=[rTS[bk]], writes=[res("TP", tab)])
            if ti < 3:
                for c, off in ((0, 64), (1, NTAB - 64)):
                    dma(dict(out=EMA[:, ti, h, c * 128:(c + 1) * 128], in_=bass.AP(TP, tab * 128 * NTAB + off, [[NTAB - 1, 128], [1, 128]])),
                        reads=[res("TP", tab)], parts=[r_const])
            else:
                for c, off in ((0, 128), (1, 0), (2, NTAB - 128)):
                    dma(dict(out=EMB[:, h, c * 128:(c + 1) * 128], in_=bass.AP(TP, tab * 128 * NTAB + off, [[NTAB - 1, 128], [1, 128]])),
                        reads=[res("TP", tab)], parts=[r_const])

    FMB = 18

    def inproj_tile(t, XS, rXS, Win, rWin, l, bufs, part="all"):
        (HT, rHT, SQ, rSQ, RS, rRS, ZS, rZS, GS, rGS, VS, rVS) = bufs
        t0 = t * 512
        if part in ("all", "norm"):
            rmsnorm_tile(XS, rXS, HT, rHT, lambda c: gA[:, l, c:c + 1], 512, SQ, rSQ, RS, rRS, 0)
        if part == "norm":
            return
        for b in range(FMB):
            bk = 1 + (b % 2)
            for k in range(8):
                s.op("pe", "matmul", dict(out=PSB[bk][:, :], lhsT=Win[:, k, b * 128:(b + 1) * 128], rhs=HT[:, k, :], start=(k == 0), stop=(k == 7)),
                     reads=[rHT, rWin], writes=[rPS[bk]])
            cast_copy(ZS[:, b, :], PSB[bk][:, :], reads=[rPS[bk]], parts=[rZS], engs=("act", "dve"))
        dma(dict(out=ZT.ap()[:, PADK + t0:PADK + t0 + 512].rearrange("(b p) t -> p b t", p=128), in_=ZS[:]),
            reads=[rZS], writes=[res("ZT", t)])
        for k in range(8):
            s.op("pe", "matmul", dict(out=PSB[3][0:24, :], lhsT=Win[:, k, 2304:2328], rhs=HT[:, k, :], start=(k == 0), stop=(k == 7)),
                 reads=[rHT, rWin], writes=[rPS[3]])
        s.op("dve", "tensor_copy", dict(out=GS[0:24, :], in_=PSB[3][0:24, :]), reads=[rPS[3]], writes=[rGS])
        dma(dict(out=GT.ap()[:, t0:t0 + 512], in_=GS[0:24, :]), reads=[rGS], writes=[res("GT", t)])
        for sub in range(4):
            bkA, bkB = 4 + 2 * (sub % 2), 5 + 2 * (sub % 2)
            tok = slice(sub * 128, (sub + 1) * 128)
            for k in range(8):
                s.op("pe", "matmul", dict(out=PSB[bkA][:, :], lhsT=HT[:, k, tok], rhs=Win[:, k, 2328:2840], start=(k == 0), stop=(k == 7)),
                     reads=[rHT, rWin], writes=[rPS[bkA]])
            for k in range(8):
                s.op("pe", "matmul", dict(out=PSB[bkB][:, 0:384], lhsT=HT[:, k, tok], rhs=Win[:, k, 2840:3224], start=(k == 0), stop=(k == 7)),
                     reads=[rHT, rWin], writes=[rPS[bkB]])
            vs, rvs = VS[sub % 2], rVS[sub % 2]
            s.op("act", "activation", dict(out=vs[:, 0:8, 0:64], in_=PSB[bkA][:, :].rearrange("p (h e) -> p h e", e=64), func=AF.Copy), reads=[rPS[bkA]], parts=[rvs])
            s.op("dve", "tensor_copy", dict(out=vs[:, 8:14, 0:64], in_=PSB[bkB][:, 0:384].rearrange("p (h e) -> p h e", e=64)),
                 reads=[rPS[bkB]], parts=[rvs])
            r0 = PADK + t0 + sub * 128
            dma(dict(out=VT.ap()[r0:r0 + 128], in_=vs[:]), reads=[rvs], writes=[res("VT", t, sub)])

    def inproj_bufs():
        HT = ar.alloc([128, 8, 512], BF16, "HT"); SQ = ar.alloc([128, 8, 512], BF16, "SQ")
        RS = ar.alloc([128, 512], F32, "RS"); ZS = ar.alloc([128, FMB, 512], BF16, "ZS")
        GS = ar.alloc([128, 512], F32, "GS")
        VS = [ar.alloc([128, 14, 128], BF16, "VS") for _ in range(2)]
        rVS = [Res(), Res()]
        for v, rv in zip(VS, rVS):
            s.op("pool", "memset", dict(ap=v[:], constant=1.0), writes=[rv])
        return (HT, Res(), SQ, Res(), RS, Res(), ZS, Res(), GS, Res(), VS, rVS)

    def load_win(l):
        Win = ar.alloc([128, 8, DIN], BF16, "Win")
        rWin = Res()
        load_weight(Win, rWin, win_in.ap()[l], D, DIN)
        return Win, rWin

    def phase_inproj0():
        phase()
        Win, rWin = load_win(0)
        bufs = inproj_bufs()
        XSs = [ar.alloc([128, 8, 512], F32, "XS") for _ in range(2)]
        rXSs = [Res(), Res()]

        def ld(t):
            dma(dict(out=XSs[t % 2][:], in_=xT_in.ap()[:, t * 512:(t + 1) * 512].rearrange("(c p) t -> p c t", p=128)),
                writes=[rXSs[t % 2]])
        ld(0)
        for t in range(NT512):
            if t + 1 < NT512:
                ld(t + 1)
            dma(dict(out=XT.ap()[:, t * 512:(t + 1) * 512].rearrange("(c p) t -> p c t", p=128), in_=XSs[t % 2][:]),
                reads=[rXSs[t % 2]], writes=[res("XT", t)])
            inproj_tile(t, XSs[t % 2], rXSs[t % 2], Win, rWin, 0, bufs)

    def phase_gates(l):
        phase()
        NC_ = NT128
        G = ar.alloc([128, 24, 128], F32, "G")
        L = ar.alloc([128, 24, 128], F32, "L")
        CA = ar.alloc([128, 12, 128], F32, "CA")
        CB = ar.alloc([128, 12, 128], F32, "CB")
        EGS = ar.alloc([128, 36, 128], F32, "EGS")
        DC = ar.alloc([128, 12], F32, "DC")
        rG, rL, rCA, rCB, rEGS, rDC = [Res() for _ in range(6)]
        P = slice(0, NC_)
        dma(dict(out=G[P], in_=GT.ap().rearrange("g (c s) -> c g s", s=128)),
            reads=[res("GT", t) for t in range(NT512)], writes=[rG])
        s.op("dve", "tensor_tensor", dict(out=G[P], in0=G[P], in1=gb[P, l, :].unsqueeze(2).to_broadcast([NC_, 24, 128]), op=ALU.add),
             reads=[rG, r_const], writes=[rG])
        s.op("act", "activation", dict(out=L[P], in_=G[P], func=AF.Exp, scale=-1.0), reads=[rG], writes=[rL])
        s.op("act", "activation", dict(out=CA[P, 0:6, :], in_=L[P, 6:12, :], func=AF.Ln, bias=1.0), reads=[rL], writes=[rCA])
        s.op("act", "activation", dict(out=CA[P, 6:12, :], in_=L[P, 18:24, :], func=AF.Ln, bias=1.0), reads=[rL], writes=[rCA])
        src, rsrc, dst, rdst = CA, rCA, CB, rCB
        k = 1
        while k < 128:
            s.op("dve", "tensor_tensor", dict(out=dst[P, 0:6, k:128], in0=src[P, 0:6, k:128], in1=src[P, 0:6, 0:128 - k], op=ALU.add),
                 reads=[rsrc], writes=[rdst])
            s.op("dve", "tensor_copy", dict(out=dst[P, 0:6, 0:k], in_=src[P, 0:6, 0:k]), reads=[rsrc], parts=[rdst])
            s.op("pool", "tensor_tensor", dict(out=dst[P, 6:12, 0:128 - k], in0=src[P, 6:12, 0:128 - k], in1=src[P, 6:12, k:128], op=ALU.add),
                 reads=[rsrc], parts=[rdst])
            s.op("pool", "tensor_copy", dict(out=dst[P, 6:12, 128 - k:128], in_=src[P, 6:12, 128 - k:128]), reads=[rsrc], parts=[rdst])
            src, rsrc, dst, rdst = dst, rdst, src, rsrc
            k *= 2
        CS, rCS, TMP, rTMP = src, rsrc, dst, rdst
        s.op("act", "activation", dict(out=EGS[P, 0:12, :], in_=CS[P], func=AF.Exp, scale=-1.0), reads=[rCS], parts=[rEGS])
        s.op("dve", "tensor_tensor", dict(out=TMP[P, 0:6, :], in0=CS[P, 0:6, :], in1=G[P, 0:6, :], op=ALU.add), reads=[rCS, rG], writes=[rTMP])
        s.op("dve", "tensor_tensor", dict(out=TMP[P, 6:12, :], in0=CS[P, 6:12, :], in1=G[P, 12:18, :], op=ALU.add), reads=[rCS, rG], parts=[rTMP])
        s.op("act", "activation", dict(out=EGS[P, 12:24, :], in_=TMP[P], func=AF.Exp, bias=math.log(0.125)), reads=[rTMP], parts=[rEGS])
        s.op("dve", "tensor_tensor", dict(out=TMP[P, 0:6, :], in0=TMP[P, 0:6, :], in1=CS[P, 0:6, 127:128].to_broadcast([NC_, 6, 128]), op=ALU.subtract),
             reads=[rCS, rTMP], writes=[rTMP])
        s.op("dve", "tensor_tensor", dict(out=TMP[P, 6:12, :], in0=TMP[P, 6:12, :], in1=CS[P, 6:12, 0:1].to_broadcast([NC_, 6, 128]), op=ALU.subtract),
             reads=[rCS, rTMP], writes=[rTMP])
        s.op("act", "activation", dict(out=EGS[P, 24:36, :], in_=TMP[P], func=AF.Exp, bias=math.log(0.125)), reads=[rTMP], parts=[rEGS])
        s.op("act", "activation", dict(out=DC[P, 0:6], in_=CS[P, 0:6, 127], func=AF.Exp, scale=-1.0), reads=[rCS], writes=[rDC])
        s.op("act", "activation", dict(out=DC[P, 6:12], in_=CS[P, 6:12, 0], func=AF.Exp, scale=-1.0), reads=[rCS], parts=[rDC])
        dma(dict(out=EG.ap().rearrange("r (c s) -> c r s", s=128), in_=EGS[P]), reads=[rEGS], writes=[res("EG")])
        DC2 = ar.alloc([128, 2, 6], F32, "DC2"); rDC2 = Res()
        for hh in range(2):
            s.op("dve", "tensor_copy", dict(out=DC2[P, hh, :], in_=DC[P, hh:12:2]), reads=[rDC], parts=[rDC2])
        dma(dict(out=DECD.ap().rearrange("h c k -> c h k"), in_=DC2[P, :, :]), reads=[rDC2], writes=[res("DECD")])

    def phase_attnA(l):
        phase()
        QTm = [ar.alloc([128, 3, 2048], BF16, "QTm") for _ in range(2)]; rQT = Res()
        for hh in range(2):
            s.op("pool", "memset", dict(ap=QTm[hh][(1 - hh) * 64:(2 - hh) * 64, :, :], constant=0.0), parts=[rQT])
        KD = {4: ar.alloc([128, 3, 4, 1024], BF16, "KD4"), 16: ar.alloc([128, 3, 16, 256], BF16, "KD16")}
        rKD = {4: Res(), 16: Res()}
        KT = ar.alloc([128, 3, 4096], BF16, "KT"); rKT = Res()
        ACC = ar.alloc([128, 6, 2048], F32, "ACC"); rACCg = {0: Res(), 4: Res()}
        NV = 6
        VCH = [ar.alloc([128, 6, 128], BF16, "VCH") for _ in range(NV)]; rV = [Res() for _ in range(NV)]
        NE = 6
        PT_ = [ar.alloc([128, 512], BF16, "PT") for _ in range(NE)]; rP = [Res() for _ in range(NE)]
        DN = ar.alloc([128, 2048], F32, "DN"); rDN = Res()
        YA = [ar.alloc([128, 2048], BF16, "YA") for _ in range(2)]; rYA = [Res(), Res()]
        zt_all = [res("ZT", t) for t in range(NT512)]
        vt_all = [res("VT", t, sub) for t in range(NT512) for sub in range(4)] + [res("VTpad")]
        vi = [0]
        ei = [0]
        sbank = [0]
        obank = [0]
        LAG = 2
        for sb in range(NSB):
            c0 = sb * 2048
            for b in range(3):
                for hh in range(2):
                    dma(dict(out=QTm[hh][hh * 64:(hh + 1) * 64, b, :], in_=ZT.ap()[b * 128 + hh * 64:b * 128 + (hh + 1) * 64, PADK + c0:PADK + c0 + 2048]),
                        reads=zt_all, parts=[rQT])
                dma(dict(out=KT[:, b, :], in_=ZT.ap()[(3 + b) * 128:(4 + b) * 128, c0:c0 + 4096]),
                    reads=zt_all + [res("ZTpad")], parts=[rKT])
            for dd in (4, 16):
                for b in range(3):
                    en_ = ("act", "dve", "pool")[(b + (0 if dd == 4 else 1)) % 3]
                    if en_ == "act":
                        s.op("act", "activation", dict(out=KD[dd][:, b, :, :], in_=KT[:, b, :].rearrange("p (u r) -> p r u", r=dd), func=AF.Copy),
                             reads=[rKT], parts=[rKD[dd]])
                    else:
                        s.op(en_, "tensor_copy", dict(out=KD[dd][:, b, :, :], in_=KT[:, b, :].rearrange("p (u r) -> p r u", r=dd)), reads=[rKT], parts=[rKD[dd]])
            steps = []
            for pi, d in enumerate((1, 4, 16)):
                ntile = 2048 // d // 128
                for r in range(d):
                    for j0 in range(ntile):
                        steps.append((pi, d, r, j0, ntile))
            pend = []

            def emit_S(st, b):
                pi, d, r, j0 = st["pi"], st["d"], st["r"], st["j0"]
                bk = sbank[0] % 4
                sbank[0] += 1
                for hh in range(2):
                    s.op("pe", "matmul", dict(out=PSB[bk][:, hh * 256:(hh + 1) * 256], lhsT=ident_bf[:], rhs=EMA[:, pi, 2 * b + hh, :], start=True, stop=False),
                         reads=[r_const], writes=[rPS[bk]])
                    for c in range(2):
                        if d == 1:
                            k0 = 1024 + 128 * (j0 + c) - 64
                            lhs = KT[:, b, k0:k0 + 128]
                        else:
                            u0 = 128 * (j0 + c) - 64 + 1024 // d
                            lhs = KD[d][:, b, r, u0:u0 + 128]
                        s.op("pe", "matmul", dict(out=PSB[bk][:, hh * 256 + c * 128:hh * 256 + (c + 1) * 128], lhsT=lhs, rhs=QTm[hh][:, b, st["qs"]],
                                                  start=False, stop=(c == 1)), reads=[rKT, rQT] + ([rKD[d]] if d > 1 else []), writes=[rPS[bk]])
                es = ei[0] % NE
                ei[0] += 1
                s.op("act", "activation", dict(out=PT_[es][:], in_=PSB[bk][:, :], func=AF.Exp, scale=0.125), reads=[rPS[bk]], writes=[rP[es]])
                return es

            def emit_PV(st, b, es):
                h0, nh = (0, 4) if b < 2 else (4, 2)
                bk = st["ob"][0 if b < 2 else 1]
                for hh in range(2):
                    h = 2 * b + hh
                    for c in range(2):
                        s.op("pe", "matmul", dict(out=PSB[bk][:, (h - h0) * 128:(h - h0 + 1) * 128], lhsT=VCH[st["v"][c]][:, h, :],
                                                  rhs=PT_[es][:, hh * 256 + c * 128:hh * 256 + (c + 1) * 128], start=(c == 0), stop=(c == 1)),
                             reads=[rV[st["v"][c]], rP[es]], writes=[rPS[bk]])
                if b in (1, 2):
                    src_ = PSB[bk][:, 0:nh * 128].rearrange("p (h q) -> p h q", h=nh)
                    qs = st["qs"]
                    if st["pi"] == 0:
                        s.op("dve", "tensor_copy", dict(out=ACC[:, h0:h0 + nh, qs], in_=src_), reads=[rPS[bk]], writes=[rACCg[h0]])
                    else:
                        s.op("dve", "tensor_tensor", dict(out=ACC[:, h0:h0 + nh, qs], in0=src_, in1=ACC[:, h0:h0 + nh, qs], op=ALU.add),
                             reads=[rPS[bk]], writes=[rACCg[h0]])

            vslot = {}
            for (pi, d, r, j0, ntile) in steps:
                def loadv(m):
                    key = (pi, r, m)
                    if key in vslot:
                        return vslot[key]
                    sl = vi[0] % NV
                    vi[0] += 1
                    row0 = PADK + sb * 2048 + r + d * (128 * m - 64)
                    dma(dict(out=VCH[sl][:], in_=VT.ap()[row0:row0 + 127 * d + 1:d, 0:6, :]), reads=vt_all, writes=[rV[sl]])
                    vslot[key] = sl
                    return sl
                v0 = loadv(j0)
                v1 = loadv(j0 + 1)
                q0 = r + d * 128 * j0
                obp = 4 + 2 * (obank[0] % 2)
                obank[0] += 1
                st = dict(pi=pi, d=d, r=r, j0=j0, qs=slice(q0, q0 + 127 * d + 1, d), v=(v0, v1), ob=(obp, obp + 1))
                for b in range(3):
                    es = emit_S(st, b)
                    pend.append((st, b, es))
                    if len(pend) > LAG:
                        emit_PV(*pend.pop(0))
            while pend:
                emit_PV(*pend.pop(0))
            for h in range(6):
                ya, rya = YA[h % 2], rYA[h % 2]
                s.op("act", "activation", dict(out=DN[0:64, :], in_=ACC[64:128, h, :], func=AF.Ln), reads=[rACCg[0], rACCg[4]], writes=[rDN])
                s.op("act", "activation", dict(out=DN[0:64, :], in_=DN[0:64, :], func=AF.Exp, scale=-1.0), reads=[rDN], writes=[rDN])
                s.op("dve", "tensor_tensor", dict(out=ya[0:64, :], in0=ACC[0:64, h, :], in1=DN[0:64, :], op=ALU.mult),
                     reads=[rDN, rACCg[0], rACCg[4]], writes=[rya])
                dma(dict(out=YT.ap()[h * 64:(h + 1) * 64, sb * 2048:(sb + 1) * 2048], in_=ya[0:64, :]),
                    reads=[rya], writes=[res("YT", "A", h, sb)])

    def phase_attnB(l):
        phase()
        NB_ = 2
        QB = [[ar.alloc([128, 2, 512], BF16, "QB") for _ in range(2)] for _ in range(NB_)]; rQB = [Res() for _ in range(NB_)]
        for sl_ in range(NB_):
            for hh in range(2):
                s.op("pool", "memset", dict(ap=QB[sl_][hh][(1 - hh) * 64:(2 - hh) * 64, :, :], constant=0.0), parts=[rQB[sl_]])
        KB = [ar.alloc([128, 768], BF16, "KB") for _ in range(NB_)]; rKB = [Res() for _ in range(NB_)]
        VB = [ar.alloc([128, 6, 2, 128], BF16, "VB") for _ in range(NB_)]; rVB = [Res() for _ in range(NB_)]
        NE = 6
        PT_ = [ar.alloc([128, 384], BF16, "PTb") for _ in range(NE)]; rP = [Res() for _ in range(NE)]
        DN = [ar.alloc([128, 512], F32, "DNb") for _ in range(2)]; rDN = [Res(), Res()]
        YB = [ar.alloc([128, 512], BF16, "YB") for _ in range(2)]; rYB = [Res(), Res()]
        zt_all = [res("ZT", t) for t in range(NT512)] + [res("ZTpad")]
        vt_all = [res("VT", t, sub) for t in range(NT512) for sub in range(4)] + [res("VTpad")]

        def ld(g):
            sl = g % NB_
            t0 = g * 512
            for hh in range(2):
                dma(dict(out=QB[sl][hh][hh * 64:(hh + 1) * 64, :, :],
                         in_=ZT.ap()[6 * 128:8 * 128, PADK + t0:PADK + t0 + 512].rearrange("(b p) t -> p b t", p=128)[hh * 64:(hh + 1) * 64]),
                    reads=zt_all, parts=[rQB[sl]])
            dma(dict(out=KB[sl][:], in_=ZT.ap()[8 * 128:9 * 128, PADK + t0 - 128:PADK + t0 + 640]), reads=zt_all, writes=[rKB[sl]])
            dma(dict(out=VB[sl][:], in_=VT.ap()[PADK + t0 - 128:PADK + t0 + 640, 6:8, :].rearrange("(c p) h e -> p c h e", p=128)),
                reads=vt_all, writes=[rVB[sl]])
        ld(0)
        ei = [0]
        sbank = [0]
        ob = [0]
        yi = [0]
        LAG = 2
        pend = []

        def emit_S(u):
            g, h, qt, sl = u["g"], u["h"], u["qt"], u["sl"]
            qb, hh = h % 2, h // 2
            bk = sbank[0] % 4
            sbank[0] += 1
            s.op("pe", "matmul", dict(out=PSB[bk][:, 0:384], lhsT=ident_bf[:], rhs=EMB[:, h, :], start=True, stop=False), reads=[r_const], writes=[rPS[bk]])
            for c in range(3):
                s.op("pe", "matmul", dict(out=PSB[bk][:, c * 128:(c + 1) * 128], lhsT=KB[sl][:, (qt + c) * 128:(qt + c + 1) * 128],
                                          rhs=QB[sl][hh][:, qb, qt * 128:(qt + 1) * 128], start=False, stop=(c == 2)),
                     reads=[rKB[sl], rQB[sl]], writes=[rPS[bk]])
            es = ei[0] % NE
            ei[0] += 1
            s.op("act", "activation", dict(out=PT_[es][:], in_=PSB[bk][:, 0:384], func=AF.Exp, scale=0.125), reads=[rPS[bk]], writes=[rP[es]])
            u["es"] = es

        def emit_PV(u):
            g, h, qt, sl, es, bko = u["g"], u["h"], u["qt"], u["sl"], u["es"], u["bko"]
            hh = h // 2
            for c in range(3):
                s.op("pe", "matmul", dict(out=PSB[bko][:, qt * 128:(qt + 1) * 128], lhsT=VB[sl][:, qt + c, hh, :], rhs=PT_[es][:, c * 128:(c + 1) * 128],
                                          start=(c == 0), stop=(c == 2)), reads=[rVB[sl], rP[es]], writes=[rPS[bko]])
            if qt == 3:
                y = yi[0] % 2
                yi[0] += 1
                s.op("dve", "tensor_scalar", dict(out=DN[y][0:64, :], in0=PSB[bko][64:128, :], scalar1=ES[64:128, l, h:h + 1], scalar2=None, op0=ALU.add),
                     reads=[rPS[bko], r_const], writes=[rDN[y]])
                s.op("act", "activation", dict(out=DN[y][0:64, :], in_=DN[y][0:64, :], func=AF.Ln), reads=[rDN[y]], writes=[rDN[y]])
                s.op("act", "activation", dict(out=DN[y][0:64, :], in_=DN[y][0:64, :], func=AF.Exp, scale=-1.0), reads=[rDN[y]], writes=[rDN[y]])
                s.op("dve", "tensor_tensor", dict(out=YB[y][0:64, :], in0=PSB[bko][0:64, :], in1=DN[y][0:64, :], op=ALU.mult),
                     reads=[rPS[bko], rDN[y]], writes=[rYB[y]])
                dma(dict(out=YT.ap()[384 + h * 64:384 + (h + 1) * 64, g * 512:(g + 1) * 512], in_=YB[y][0:64, :]),
                    reads=[rYB[y]], writes=[res("YT", "B", h, g)])

        for g in range(NT512):
            while pend:
                emit_PV(pend.pop(0))
            if g + 1 < NT512:
                ld(g + 1)
            sl = g % NB_
            for h in range(4):
                bko = 4 + ob[0] % 4
                ob[0] += 1
                for qt in range(4):
                    u = dict(g=g, h=h, qt=qt, sl=sl, bko=bko)
                    emit_S(u)
                    pend.append(u)
                    if len(pend) > LAG:
                        emit_PV(pend.pop(0))
        while pend:
            emit_PV(pend.pop(0))

    def phase_mlstm_pre(l):
        phase()
        NBF = 2
        XC = [ar.alloc([128, 6, 516], BF16, "XC") for _ in range(NBF)]; rXC = [Res() for _ in range(NBF)]
        EGT = [ar.alloc([128, 512], F32, "EGT") for _ in range(NBF)]; rEGT = [Res() for _ in range(NBF)]
        SELf = ar.alloc([128, 18, 128], F32, "SELf"); rSELf = Res()
        SEL = ar.alloc([128, 18, 128], BF16, "SEL"); rSEL = Res()
        s.op("pool", "memset", dict(ap=SELf[:], constant=0.0), writes=[rSELf])
        dma(dict(out=SELf[0:36, :, :], in_=sel_in.ap()), writes=[rSELf])
        s.op("dve", "tensor_copy", dict(out=SEL[:], in_=SELf[:]), reads=[rSELf], writes=[rSEL])
        EGH = [ar.alloc([128, 2, 512], BF16, "EGH") for _ in range(NBF)]; rEGH = [Res() for _ in range(NBF)]
        for sl_ in range(NBF):
            s.op("pool", "memset", dict(ap=EGH[sl_][:], constant=0.0), writes=[rEGH[sl_]])
        SL = [ar.alloc([128, 512], F32, "SL") for _ in range(4)]; rSL = [Res() for _ in range(4)]
        MS = [ar.alloc([128, 18, 512], BF16, "MS") for _ in range(2)]; rMS = [Res(), Res()]
        KTS = [ar.alloc([128, 24, 128], BF16, "KTS") for _ in range(2)]; rKTS = [Res(), Res()]
        DG = ar.alloc([128, 6, 5, 128], BF16, "DG"); rDG = Res()
        zt_all = [res("ZT", t) for t in range(NT512)] + [res("ZTpad")]
        for b6 in range(6):
            for j in range(5):
                s.op("dve" if (b6 + j) % 2 == 0 else "pool", "tensor_scalar",
                     dict(out=DG[:, b6, j, :], in0=ident_bf[:], scalar1=cw[:, l, b6, j:j + 1], scalar2=None, op0=ALU.mult), reads=[r_const], parts=[rDG])

        def ld(t):
            sl = t % NBF
            t0 = t * 512
            dma(dict(out=XC[sl][:], in_=ZT.ap()[9 * 128:15 * 128, PADK + t0 - 2:PADK + t0 + 514].rearrange("(b p) t -> p b t", p=128)),
                reads=zt_all, writes=[rXC[sl]])
            dma(dict(out=EGT[sl][0:36, :], in_=EG.ap()[:, t0:t0 + 512]), reads=[res("EG")], writes=[rEGT[sl]])
        ld(0)
        bci = [0]
        ai = [0]
        pr = [0]
        for t in range(NT512):
            if t + 1 < NT512:
                ld(t + 1)
            sl = t % NBF
            ms, rms = MS[t % 2], rMS[t % 2]
            kts, rkts = KTS[t % 2], rKTS[t % 2]
            s.op("act", "activation", dict(out=EGH[sl][0:36, 0, :], in_=EGT[sl][0:36, :], func=AF.Copy), reads=[rEGT[sl]], writes=[rEGH[sl]])
            s.op("dve", "tensor_tensor", dict(out=EGT[sl][0:36, :], in0=EGT[sl][0:36, :], in1=EGH[sl][0:36, 0, :], op=ALU.subtract),
                 reads=[rEGH[sl]], writes=[rEGT[sl]])
            s.op("dve", "tensor_copy", dict(out=EGH[sl][0:36, 1, :], in_=EGT[sl][0:36, :]), reads=[rEGT[sl]], writes=[rEGH[sl]])
            for b6 in range(6):
                a = ai[0] % 4
                ai[0] += 1
                bk = a % 3
                for j in range(5):
                    s.op("pe", "matmul", dict(out=PSB[bk][:, :], lhsT=DG[:, b6, j, :], rhs=XC[sl][:, b6, j:j + 512], start=(j == 0), stop=(j == 4)),
                         reads=[rXC[sl], rDG], writes=[rPS[bk]])
                s.op("act", "activation", dict(out=SL[a][:], in_=PSB[bk][:, :], func=AF.Silu), reads=[rPS[bk]], writes=[rSL[a]])
                for dr in range(2):
                    qtys = (0,) if b6 < 3 else (1, 2)
                    for qty in qtys:
                        idx = dr * 9 + qty * 3 + (b6 % 3)
                        bb = (3, 7)[bci[0] % 2]
                        bci[0] += 1
                        s.op("pe", "matmul", dict(out=PSB[bb][:, :], lhsT=SEL[:, idx, :], rhs=EGH[sl][:, 0, :], start=True, stop=False),
                             reads=[rSEL, rEGH[sl]], writes=[rPS[bb]])
                        s.op("pe", "matmul", dict(out=PSB[bb][:, :], lhsT=SEL[:, idx, :], rhs=EGH[sl][:, 1, :], start=False, stop=True),
                             reads=[rSEL, rEGH[sl]], writes=[rPS[bb]])
                        s.op("dve", "tensor_tensor", dict(out=ms[:, idx, :], in0=SL[a][:], in1=PSB[bb][:, :], op=ALU.mult),
                             reads=[rSL[a], rPS[bb]], parts=[rms])
            dma(dict(out=MT.ap()[:, t * 512:(t + 1) * 512].rearrange("(b p) t -> p b t", p=128), in_=ms[:]),
                reads=[rms], writes=[res("MT", t)])
            for i in range(24):
                dr, rem = divmod(i, 12)
                sub, b = divmod(rem, 3)
                bk = 4 + i // 8
                pT = PSB[bk][:, :].bitcast(BF16)
                s.op("pe", "transpose", dict(out=pT[:, (i % 8) * 128:(i % 8 + 1) * 128], in_=ms[:, dr * 9 + 6 + b, sub * 128:(sub + 1) * 128], identity=ident_bf[:]),
                     reads=[rms, r_const], writes=[rPS[bk]])
                if i % 8 == 7:
                    s.op("act", "activation", dict(out=kts[:, i - 7:i + 1, :].rearrange("p a e -> p (a e)"), in_=pT[:, :], func=AF.Copy),
                         reads=[rPS[bk]], parts=[rkts])
            for dr in range(2):
                dma(dict(out=MTK.ap()[dr, t * 512:(t + 1) * 512].rearrange("(sub p) b e -> p sub b e", p=128),
                         in_=kts[:, dr * 12:(dr + 1) * 12, :].rearrange("p (sub b) e -> p sub b e", b=3)),
                    reads=[rkts], writes=[res("MTK", dr, t)])

    def phase_mlstm_rec(l):
        phase()
        NBF = 2
        QK = [[ar.alloc([128, 6, 128], BF16, "QK") for _ in range(NBF)] for _ in range(2)]
        rQK = [[Res() for _ in range(NBF)] for _ in range(2)]
        VV = [[ar.alloc([128, 6, 128], BF16, "VV") for _ in range(NBF)] for _ in range(2)]
        QM = [[[ar.alloc([128, 3, 128], BF16, "QM") for _ in range(2)] for _ in range(NBF)] for _ in range(2)]
        KM = [[[ar.alloc([128, 3, 128], BF16, "KM") for _ in range(2)] for _ in range(NBF)] for _ in range(2)]
        for dr_ in range(2):
            for sl_ in range(NBF):
                for hh in range(2):
                    s.op("pool", "memset", dict(ap=QM[dr_][sl_][hh][(1 - hh) * 64:(2 - hh) * 64, :, :], constant=0.0), parts=[rQK[dr_][sl_]])
                    s.op("pool", "memset", dict(ap=KM[dr_][sl_][hh][:, :, (1 - hh) * 64:(2 - hh) * 64], constant=0.0), parts=[rQK[dr_][sl_]])
        C32 = [ar.alloc([128, 3, 128], F32, "C32") for _ in range(2)]
        Cbf = [ar.alloc([128, 3, 128], BF16, "Cbf") for _ in range(2)]
        rC32 = [Res(), Res()]
        rCbf = [Res(), Res()]
        DECH = ar.alloc([128, NT128, 6], F32, "DECH"); rDECH = Res()
        PM = [ar.alloc([128, 6, 128], BF16, "PM") for _ in range(2)]; rPM = [Res(), Res()]
        AB = [ar.alloc([128, 512], F32, "AB") for _ in range(4)]; rAB = [Res() for _ in range(4)]
        HS = [ar.alloc([128, 6, 128], F32, "HS") for _ in range(2)]; rHS = [Res(), Res()]
        mt_all = [res("MT", t) for t in range(NT512)]
        mtk_all = [res("MTK", dr, t) for dr in range(2) for t in range(NT512)]
        vt_all = [res("VT", t, sub) for t in range(NT512) for sub in range(4)]
        for hh in range(2):
            dma(dict(out=DECH[hh * 64:(hh + 1) * 64, :, :].rearrange("p c k -> p (c k)"), in_=bass.AP(DECD, hh * NT128 * 6, [[0, 64], [1, NT128 * 6]])),
                reads=[res("DECD")], parts=[rDECH])
        for dr in range(2):
            s.op("dve", "memset", dict(ap=C32[dr][:], constant=0.0), writes=[rC32[dr]])
            s.op("pool", "memset", dict(ap=Cbf[dr][:], constant=0.0), writes=[rCbf[dr]])

        def tile_of(dr, g):
            return g if dr == 0 else NT128 - 1 - g

        def ld(dr, g):
            sl = g % NBF
            j = tile_of(dr, g)
            r = rQK[dr][sl]
            mtv = MT.ap()[dr * 9 * 128:(dr + 1) * 9 * 128, j * 128:(j + 1) * 128].rearrange("(b p) t -> p b t", p=128)
            dma(dict(out=QK[dr][sl][:], in_=mtv[:, 0:6, :]), reads=mt_all, parts=[r])
            for hh in range(2):
                dma(dict(out=QM[dr][sl][hh][hh * 64:(hh + 1) * 64, :, :], in_=mtv[hh * 64:(hh + 1) * 64, 0:3, :]), reads=mt_all, parts=[r])
                dma(dict(out=KM[dr][sl][hh][:, :, hh * 64:(hh + 1) * 64], in_=MTK.ap()[dr, j * 128:(j + 1) * 128, :, hh * 64:(hh + 1) * 64]),
                    reads=mtk_all, parts=[r])
            dma(dict(out=VV[dr][sl][:], in_=VT.ap()[PADK + j * 128:PADK + (j + 1) * 128, 8:14, :]), reads=vt_all, parts=[r])
        for dr in range(2):
            ld(dr, 0)
        for g in range(NT128):
            for dr in range(2):
                if g + 1 < NT128:
                    ld(dr, g + 1)
            sl = g % NBF
            for dr in range(2):
                qk, rqk, vv, qm, km = QK[dr][sl], rQK[dr][sl], VV[dr][sl], QM[dr][sl], KM[dr][sl]
                bA, bB, bD, bE = 4 * dr, 4 * dr + 1, 4 * dr + 2, 4 * dr + 3
                for h in range(6):
                    b, hh = h // 2, h % 2
                    bk, c0 = (bA, h * 128) if h < 4 else (bB, (h - 4) * 128)
                    s.op("pe", "matmul", dict(out=PSB[bk][:, c0:c0 + 128], lhsT=qk[:, 3 + b, :], rhs=qm[hh][:, b, :], start=True, stop=True),
                         reads=[rqk], writes=[rPS[bk]])
                for b in range(3):
                    bk, c0 = (bB, 256 + b * 128) if b < 2 else (bE, 256)
                    for hh in range(2):
                        s.op("pe", "matmul", dict(out=PSB[bk][:, c0:c0 + 128], lhsT=km[hh][:, b, :], rhs=vv[:, 2 * b + hh, :], start=(hh == 0), stop=(hh == 1)),
                             reads=[rqk], writes=[rPS[bk]])
            for dr in range(2):
                bA, bB = 4 * dr, 4 * dr + 1
                mask = maskF if dr == 0 else maskB
                s.op("dve", "tensor_tensor", dict(out=PM[dr][:, 0:4, :], in0=PSB[bA][:, :].rearrange("p (h t) -> p h t", h=4),
                                                  in1=mask[:].unsqueeze(1).to_broadcast([128, 4, 128]), op=ALU.mult),
                     reads=[rPS[bA], r_const], writes=[rPM[dr]])
                s.op("dve", "tensor_tensor", dict(out=PM[dr][:, 4:6, :], in0=PSB[bB][:, 0:256].rearrange("p (h t) -> p h t", h=2),
                                                  in1=mask[:].unsqueeze(1).to_broadcast([128, 2, 128]), op=ALU.mult),
                     reads=[rPS[bB], r_const], parts=[rPM[dr]])
            for dr in range(2):
                rqk, vv, qm = rQK[dr][sl], VV[dr][sl], QM[dr][sl]
                bD, bE = 4 * dr + 2, 4 * dr + 3
                for h in range(6):
                    b, hh = h // 2, h % 2
                    bk, c0 = (bD, h * 128) if h < 4 else (bE, (h - 4) * 128)
                    s.op("pe", "matmul", dict(out=PSB[bk][:, c0:c0 + 128], lhsT=vv[:, h, :], rhs=PM[dr][:, h, :], start=True, stop=False),
                         reads=[rqk, rPM[dr]], writes=[rPS[bk]])
                    s.op("pe", "matmul", dict(out=PSB[bk][:, c0:c0 + 128], lhsT=Cbf[dr][:, b, :], rhs=qm[hh][:, b, :], start=False, stop=True),
                         reads=[rCbf[dr], rqk], writes=[rPS[bk]])
            items = []
            for dr in range(2):
                bD, bE = 4 * dr + 2, 4 * dr + 3
                items.append((dr, bD, 0, 4, 2 * dr))
                items.append((dr, bE, 4, 2, 2 * dr + 1))
            for dr in range(2):
                j = tile_of(dr, g)
                bB, bE = 4 * dr + 1, 4 * dr + 3
                s.op("dve", "tensor_tensor", dict(out=C32[dr][:], in0=C32[dr][:], in1=DECH[:, j, dr * 3:dr * 3 + 3].unsqueeze(2).to_broadcast([128, 3, 128]), op=ALU.mult),
                     reads=[rDECH], writes=[rC32[dr]])
                s.op("dve", "tensor_tensor", dict(out=C32[dr][:, 0:2, :], in0=PSB[bB][:, 256:512].rearrange("p (b e) -> p b e", b=2), in1=C32[dr][:, 0:2, :], op=ALU.add),
                     reads=[rPS[bB]], writes=[rC32[dr]])
                s.op("dve", "tensor_tensor", dict(out=C32[dr][:, 2, :], in0=PSB[bE][:, 256:384], in1=C32[dr][:, 2, :], op=ALU.add),
                     reads=[rPS[bE]], writes=[rC32[dr]])
            for dr in range(2):
                s.op("act", "activation", dict(out=Cbf[dr][:], in_=C32[dr][:], func=AF.Copy), reads=[rC32[0], rC32[1]], writes=[rCbf[dr]])
            for (dr, ob, h0, nh, a) in items:
                W = nh * 128
                s.op("act", "activation", dict(out=AB[a][0:64, 0:W], in_=PSB[ob][64:128, 0:W], func=AF.Abs), reads=[rPS[ob], rC32[0], rC32[1]], writes=[rAB[a]])
            for (dr, ob, h0, nh, a) in items:
                W = nh * 128
                s.op("dve", "tensor_scalar_max", dict(out=AB[a][0:64, 0:W], in0=AB[a][0:64, 0:W], scalar1=1.0), reads=[rAB[a]], writes=[rAB[a]])
            for (dr, ob, h0, nh, a) in items:
                W = nh * 128
                s.op("act", "activation", dict(out=AB[a][0:64, 0:W], in_=AB[a][0:64, 0:W], func=AF.Ln), reads=[rAB[a]], writes=[rAB[a]])
            for (dr, ob, h0, nh, a) in items:
                W = nh * 128
                s.op("act", "activation", dict(out=AB[a][0:64, 0:W], in_=AB[a][0:64, 0:W], func=AF.Exp, scale=-1.0), reads=[rAB[a]], writes=[rAB[a]])
            for (dr, ob, h0, nh, a) in items:
                W = nh * 128
                hsb, rhs_ = HS[dr], rHS[dr]
                s.op("dve", "tensor_tensor", dict(out=hsb[0:64, h0:h0 + nh, :], in0=PSB[ob][0:64, 0:W].rearrange("p (h t) -> p h t", h=nh),
                                                  in1=AB[a][0:64, 0:W].rearrange("p (h t) -> p h t", h=nh), op=ALU.mult),
                     reads=[rPS[ob], rAB[a]], parts=[rhs_])
            for dr in range(2):
                j = tile_of(dr, g)
                dma(dict(out=HD.ap()[dr, :, j * 128:(j + 1) * 128].rearrange("(h e) t -> e h t", e=64), in_=HS[dr][0:64, :, :]),
                    reads=[rHS[dr]], writes=[res("HD", dr, j)])

    def phase_mlstm_post(l):
        phase()
        NBF = 2
        HF = [ar.alloc([128, 3, 512], F32, "HF") for _ in range(NBF)]; rHF = [Res() for _ in range(NBF)]
        HB = [ar.alloc([128, 3, 512], F32, "HB") for _ in range(NBF)]; rHB = [Res() for _ in range(NBF)]
        OC = [ar.alloc([128, 3, 512], BF16, "OC") for _ in range(NBF)]; rOC = [Res() for _ in range(NBF)]
        SQ = ar.alloc([128, 3, 512], BF16, "SQm"); rSQ = Res()
        RS = ar.alloc([128, 3, 512], F32, "RSm"); rRS = Res()
        SG = ar.alloc([128, 3, 512], F32, "SGm"); rSG = Res()
        YS = [ar.alloc([128, 3, 512], BF16, "YSm") for _ in range(2)]; rYS = [Res(), Res()]
        hd_all = [res("HD", dr, j) for dr in range(2) for j in range(NT128)]
        zt_all = [res("ZT", t) for t in range(NT512)]

        def ld(t):
            sl = t % NBF
            dma(dict(out=HF[sl][:], in_=HD.ap()[0, :, t * 512:(t + 1) * 512].rearrange("(b p) t -> p b t", p=128)), reads=hd_all, writes=[rHF[sl]])
            dma(dict(out=HB[sl][:], in_=HD.ap()[1, :, t * 512:(t + 1) * 512].rearrange("(b p) t -> p b t", p=128)), reads=hd_all, writes=[rHB[sl]])
            dma(dict(out=OC[sl][:], in_=ZT.ap()[15 * 128:18 * 128, PADK + t * 512:PADK + (t + 1) * 512].rearrange("(b p) t -> p b t", p=128)),
                reads=zt_all, writes=[rOC[sl]])
        ld(0)
        for t in range(NT512):
            if t + 1 < NT512:
                ld(t + 1)
            sl = t % NBF
            ys, rys = YS[t % 2], rYS[t % 2]
            s.op("dve", "tensor_tensor", dict(out=HF[sl][:], in0=HF[sl][:], in1=HB[sl][:], op=ALU.add), reads=[rHF[sl], rHB[sl]], writes=[rHF[sl]])
            s.op("act", "activation", dict(out=SQ[:], in_=HF[sl][:], func=AF.Square), reads=[rHF[sl]], writes=[rSQ])
            for b in range(3):
                s.op("pe", "matmul", dict(out=PSB[b][:, :], lhsT=bd_bf[:], rhs=SQ[:, b, :], start=True, stop=True), reads=[rSQ, r_const], writes=[rPS[b]])
            for b in range(3):
                s.op("act", "activation", dict(out=RS[:, b, :], in_=PSB[b][:, :], func=AF.Ln, scale=1.0 / 64, bias=EPS), reads=[rPS[b]], parts=[rRS])
            s.op("act", "activation", dict(out=RS[:], in_=RS[:], func=AF.Exp, scale=-0.5), reads=[rRS], writes=[rRS])
            s.op("act", "activation", dict(out=SG[:], in_=OC[sl][:], func=AF.Sigmoid), reads=[rOC[sl]], writes=[rSG])
            for b in range(3):
                s.op("dve", "scalar_tensor_tensor", dict(out=RS[:, b, :], in0=HF[sl][:, b, :], scalar=gM[:, l, b:b + 1], in1=RS[:, b, :], op0=ALU.mult, op1=ALU.mult), reads=[rHF[sl], rRS, r_const], writes=[rRS])
            s.op("pool", "tensor_tensor", dict(out=ys[:], in0=RS[:], in1=SG[:], op=ALU.mult), reads=[rRS, rSG], writes=[rys])
            dma(dict(out=YT.ap()[640:1024, t * 512:(t + 1) * 512].rearrange("(b p) t -> p b t", p=128), in_=ys[:]),
                reads=[rys], writes=[res("YT", "C", t)])

    def phase_outproj(l):
        phase()
        Wo = ar.alloc([128, 8, D], BF16, "Wo"); rWo = Res()
        load_weight(Wo, rWo, wout_in.ap()[l], D, D)
        NBF = 2
        XS = [ar.alloc([128, 8, 512], F32, "XSo") for _ in range(NBF)]; rXS = [Res() for _ in range(NBF)]
        YS = [ar.alloc([128, 8, 512], BF16, "YSo") for _ in range(NBF)]; rYS = [Res() for _ in range(NBF)]
        yt_all = ([res("YT", "A", h, sb) for h in range(6) for sb in range(NSB)] + [res("YT", "B", h, g) for h in range(4) for g in range(NT512)]
                  + [res("YT", "C", t) for t in range(NT512)])

        def ld(t):
            sl = t % NBF
            dma(dict(out=XS[sl][:], in_=XT.ap()[:, t * 512:(t + 1) * 512].rearrange("(c p) t -> p c t", p=128)), reads=[res("XT", t)], writes=[rXS[sl]])
            dma(dict(out=YS[sl][:], in_=YT.ap()[:, t * 512:(t + 1) * 512].rearrange("(c p) t -> p c t", p=128)), reads=yt_all, writes=[rYS[sl]])
        ld(0)
        for t in range(NT512):
            if t + 1 < NT512:
                ld(t + 1)
            sl = t % NBF
            for m in range(8):
                bk = m % 4
                for k in range(8):
                    s.op("pe", "matmul", dict(out=PSB[bk][:, :], lhsT=Wo[:, k, m * 128:(m + 1) * 128], rhs=YS[sl][:, k, :], start=(k == 0), stop=(k == 7)),
                         reads=[rWo, rYS[sl]], writes=[rPS[bk]])
                s.op("dve", "tensor_tensor", dict(out=XS[sl][:, m, :], in0=PSB[bk][:, :], in1=XS[sl][:, m, :], op=ALU.add),
                     reads=[rPS[bk], rXS[sl]], writes=[rXS[sl]])
            dma(dict(out=XT.ap()[:, t * 512:(t + 1) * 512].rearrange("(c p) t -> p c t", p=128), in_=XS[sl][:]),
                reads=[rXS[sl]], writes=[res("XT", t)])

    def phase_ffn(l):
        phase()
        NJ = DFF // 128
        Wu = ar.alloc([128, 8, 2 * DFF], BF16, "Wu"); rWu = Res()
        Wd = ar.alloc([128, NJ, D], BF16, "Wd"); rWd = Res()
        mark = ar.off
        load_weight(Wu, rWu, wup_in.ap()[l], D, 2 * DFF)
        load_weight(Wd, rWd, wdn_in.ap()[l], DFF, D)
        ar.off = mark
        s.barrier()
        T = 256
        NBF = 2
        XS = [ar.alloc([128, 8, T], F32, "XSf") for _ in range(NBF)]; rXS = [Res() for _ in range(NBF)]
        HT = ar.alloc([128, 8, T], BF16, "HTf"); rHT = Res()
        SQ = ar.alloc([128, 8, T], BF16, "SQf"); rSQ = Res()
        RS = ar.alloc([128, T], F32, "RSf"); rRS = Res()
        AT = ar.alloc([128, NJ, T], BF16, "ATf"); rAT = Res()
        SGs = [ar.alloc([128, T], F32, "SGf") for _ in range(3)]; rSG = [Res() for _ in range(3)]

        def ld(t):
            sl = t % NBF
            dma(dict(out=XS[sl][:], in_=XT.ap()[:, t * T:(t + 1) * T].rearrange("(c p) t -> p c t", p=128)),
                reads=[res("XT", t // 2)], writes=[rXS[sl]])
        ld(0)
        for t in range(NT256):
            if t + 1 < NT256:
                ld(t + 1)
            sl = t % NBF
            rmsnorm_tile(XS[sl], rXS[sl], HT, rHT, lambda c: gF[:, l, c:c + 1], T, SQ, rSQ, RS, rRS, 0)
            for jb in range(NJ):
                bk = 1 + jb % 3
                for k in range(8):
                    s.op("pe", "matmul", dict(out=PSB[bk][:, 0:T], lhsT=Wu[:, k, jb * 128:(jb + 1) * 128], rhs=HT[:, k, :], start=(k == 0), stop=(k == 7)),
                         reads=[rWu, rHT], writes=[rPS[bk]])
                for k in range(8):
                    s.op("pe", "matmul", dict(out=PSB[bk][:, T:2 * T], lhsT=Wu[:, k, DFF + jb * 128:DFF + (jb + 1) * 128], rhs=HT[:, k, :], start=(k == 0), stop=(k == 7)),
                         reads=[rWu, rHT], writes=[rPS[bk]])
                sg = jb % 3
                s.op("act", "activation", dict(out=SGs[sg][:], in_=PSB[bk][:, 0:T], func=AF.Silu), reads=[rPS[bk]], writes=[rSG[sg]])
                s.op("dve", "tensor_tensor", dict(out=AT[:, jb, :], in0=PSB[bk][:, T:2 * T], in1=SGs[sg][:], op=ALU.mult),
                     reads=[rPS[bk], rSG[sg]], parts=[rAT])
            for m in range(8):
                bk = 4 + m % 4
                for jb in range(NJ):
                    s.op("pe", "matmul", dict(out=PSB[bk][:, 0:T], lhsT=Wd[:, jb, m * 128:(m + 1) * 128], rhs=AT[:, jb, :], start=(jb == 0), stop=(jb == NJ - 1)),
                         reads=[rWd, rAT], writes=[rPS[bk]])
                s.op("dve", "tensor_tensor", dict(out=XS[sl][:, m, :], in0=PSB[bk][:, 0:T], in1=XS[sl][:, m, :], op=ALU.add),
                     reads=[rPS[bk], rXS[sl]], writes=[rXS[sl]])
            dma(dict(out=XT.ap()[:, t * T:(t + 1) * T].rearrange("(c p) t -> p c t", p=128), in_=XS[sl][:]),
                reads=[rXS[sl]], writes=[res("XT2", t)])

    def phase_ple(l):
        phase()
        last = (l == depth - 1)
        Wg = ar.alloc([128, 8, D], BF16, "Wg"); rWg = Res()
        Wp = ar.alloc([128, 2, D], BF16, "Wp"); rWp = Res()
        load_weight(Wg, rWg, wpg_in.ap()[l], D, D)
        load_weight(Wp, rWp, wpp_in.ap()[l], PLE, D)
        if not last:
            Win, rWin = load_win(l + 1)
            bufs = inproj_bufs()
            HT, rHT, SQ, rSQ, RS, rRS = bufs[0:6]
        else:
            HT = ar.alloc([128, 8, 512], F32, "HTo"); rHT = Res()
            SQ = ar.alloc([128, 8, 512], BF16, "SQo"); rSQ = Res()
            RS = ar.alloc([128, 512], F32, "RSo"); rRS = Res()
        NBF = 2
        XS = [ar.alloc([128, 8, 512], F32, "XSp") for _ in range(NBF)]; rXS = [Res() for _ in range(NBF)]
        PS_ = [ar.alloc([128, 2, 512], F32, "PSp") for _ in range(NBF)]; rPS_ = [Res() for _ in range(NBF)]
        XB, rXB = SQ, rSQ
        PB = ar.alloc([128, 2, 512], BF16, "PBp"); rPB = Res()
        SGs = [ar.alloc([128, 512], F32, "SGp") for _ in range(2)]; rSG = [Res(), Res()]
        fin = []

        def ld(t):
            sl = t % NBF
            dma(dict(out=XS[sl][:], in_=XT.ap()[:, t * 512:(t + 1) * 512].rearrange("(c p) t -> p c t", p=128)),
                reads=[res("XT2", 2 * t), res("XT2", 2 * t + 1)], writes=[rXS[sl]])
            dma(dict(out=PS_[sl][:], in_=pT_in.ap()[l, :, t * 512:(t + 1) * 512].rearrange("(c p) t -> p c t", p=128)), writes=[rPS_[sl]])
        def ple_part(t):
            sl = t % NBF
            s.op("dve", "tensor_copy", dict(out=XB[:, 0:4, :], in_=XS[sl][:, 0:4, :]), reads=[rXS[sl]], parts=[rXB])
            s.op("act", "activation", dict(out=XB[:, 4:8, :], in_=XS[sl][:, 4:8, :], func=AF.Copy), reads=[rXS[sl]], parts=[rXB])
            s.op("act", "activation", dict(out=PB[:], in_=PS_[sl][:], func=AF.Copy), reads=[rPS_[sl]], writes=[rPB])
            for m in range(8):
                bk = 4 + (m % 2) * 2
                for k in range(8):
                    s.op("pe", "matmul", dict(out=PSB[bk][:, :], lhsT=Wg[:, k, m * 128:(m + 1) * 128], rhs=XB[:, k, :], start=(k == 0), stop=(k == 7)),
                         reads=[rWg, rXB], writes=[rPS[bk]])
                for k in range(2):
                    s.op("pe", "matmul", dict(out=PSB[bk + 1][:, :], lhsT=Wp[:, k, m * 128:(m + 1) * 128], rhs=PB[:, k, :], start=(k == 0), stop=(k == 1)),
                         reads=[rWp, rPB], writes=[rPS[bk + 1]])
                sg = m % 2
                s.op("act", "activation", dict(out=SGs[sg][:], in_=PSB[bk][:, :], func=AF.Sigmoid), reads=[rPS[bk]], writes=[rSG[sg]])
                s.op("dve", "tensor_tensor", dict(out=SGs[sg][:], in0=PSB[bk + 1][:, :], in1=SGs[sg][:], op=ALU.mult),
                     reads=[rPS[bk + 1], rSG[sg]], writes=[rSG[sg]])
                s.op("pool", "tensor_tensor", dict(out=XS[sl][:, m, :], in0=SGs[sg][:], in1=XS[sl][:, m, :], op=ALU.add),
                     reads=[rSG[sg], rXS[sl]], writes=[rXS[sl]])
            if not last:
                dma(dict(out=XT.ap()[:, t * 512:(t + 1) * 512].rearrange("(c p) t -> p c t", p=128), in_=XS[sl][:]),
                    reads=[rXS[sl]], writes=[res("XT", t)])

        def norm_part(t):
            sl = t % NBF
            if not last:
                inproj_tile(t, XS[sl], rXS[sl], Win, rWin, l + 1, bufs, part="norm")
            else:
                s.op("act", "activation", dict(out=SQ[:], in_=XS[sl][:], func=AF.Square), reads=[rXS[sl]], writes=[rSQ])
                for c in range(8):
                    s.op("pe", "matmul", dict(out=PSB[0][:, :], lhsT=ones_bf[:], rhs=SQ[:, c, :], start=(c == 0), stop=(c == 7)), reads=[rSQ, r_const], writes=[rPS[0]])
                s.op("act", "activation", dict(out=RS[:], in_=PSB[0][:, :], func=AF.Ln, scale=1.0 / D, bias=EPS), reads=[rPS[0]], writes=[rRS])
                s.op("act", "activation", dict(out=RS[:], in_=RS[:], func=AF.Exp, scale=-0.5), reads=[rRS], writes=[rRS])
                for c in range(8):
                    s.op("dve", "scalar_tensor_tensor", dict(out=HT[:, c, :], in0=XS[sl][:, c, :], scalar=gfin[:, c:c + 1], in1=RS[:], op0=ALU.mult, op1=ALU.mult),
                         reads=[rXS[sl], rRS, r_const], parts=[rHT])
                fin.append(dma(dict(out=out_d.ap()[:, t * 512:(t + 1) * 512].rearrange("(c p) t -> p c t", p=128), in_=HT[:]),
                               reads=[rHT], writes=[res("OUT", t)]))

        ld(0)
        if NT512 > 1:
            ld(1)
        ple_part(0)
        for t in range(NT512):
            norm_part(t)
            if t + 2 < NT512:
                ld(t + 2)
            if t + 1 < NT512:
                ple_part(t + 1)
            if not last:
                inproj_tile(t, XS[t % NBF], rXS[t % NBF], Win, rWin, l + 1, bufs, part="proj")
        return fin

    plist = [("init", phase_init, ()), ("inproj0", phase_inproj0, ())]
    for l in range(depth):
        plist += [("gates", phase_gates, (l,)), ("attnA", phase_attnA, (l,)), ("attnB", phase_attnB, (l,)),
                  ("mpre", phase_mlstm_pre, (l,)), ("mrec", phase_mlstm_rec, (l,)), ("mpost", phase_mlstm_post, (l,)),
                  ("outproj", phase_outproj, (l,)), ("ffn", phase_ffn, (l,)), ("ple", phase_ple, (l,))]
    for (name, fn, args) in plist:
        fn(*args)
        if upto is not None and name == upto:
            break
    s.emit(final_tokens=s.all_tokens())
    return nc, s


_COLS = dict(qa=0, ka=384, va=768, qb=1152, kb=1408, vb=1536, qc=1664, kc=2048, vc=2432, oc=2816, gc=3200)


def _selectors():
    sel = np.zeros((36, 18, 128), np.float32)
    for dr in range(2):
        for qty in range(3):
            for b in range(3):
                idx = dr * 9 + qty * 3 + b
                for hh in range(2):
                    sel[qty * 12 + dr * 6 + 2 * b + hh, idx, hh * 64:(hh + 1) * 64] = 1.0
    return sel


def _win_perm():
    c = _COLS
    r = lambda a, n: list(range(a, a + n))
    qb = c["qb"]
    perm = (r(c["qa"], 384) + r(c["ka"], 384) + r(qb, 64) + r(qb + 128, 64) + r(qb + 64, 64) + r(qb + 192, 64)
            + r(c["kb"], 128) + r(c["qc"], 384) + r(c["kc"], 384) + r(c["oc"], 384) + r(c["gc"], 24)
            + r(c["va"], 384) + r(c["vb"], 128) + r(c["vc"], 384))
    assert len(perm) == DIN and len(set(perm)) == DIN
    return np.array(perm)


def host_inputs(inputs, S, depth, b):
    f = lambda a: np.ascontiguousarray(np.asarray(a, dtype=np.float32))
    x = np.asarray(inputs["x"]); p = np.asarray(inputs["p"])
    bc = lambda a: np.ascontiguousarray(np.broadcast_to(np.asarray(a, np.float32)[None], (128,) + tuple(np.asarray(a).shape)))
    i = np.arange(128)
    m = {
        "xT": f(x[b].T),
        "pT": f(np.transpose(p[:depth, b], (0, 2, 1))),
        "relb": f(inputs["rel_bias"]),
        "gA": f(np.asarray(inputs["attn_norm"])[:depth].reshape(depth, 8, 128).transpose(2, 0, 1)),
        "gF": f(np.asarray(inputs["ffn_norm"])[:depth].reshape(depth, 8, 128).transpose(2, 0, 1)),
        "gfin": f(np.asarray(inputs["final_norm"]).reshape(8, 128).T),
        "gM": f(np.asarray(inputs["mlstm_norm"])[:depth].reshape(depth, 3, 128).transpose(2, 0, 1)),
        "cw": f(np.asarray(inputs["qk_conv"])[:depth].reshape(depth, 5, 6, 128).transpose(3, 0, 2, 1)),
        "gb": bc(np.asarray(inputs["gate_bias"])[:depth]),
        "sk": bc(np.asarray(inputs["sink_logits"])[:depth]),
        "w_in": f(np.asarray(inputs["w_in"])[:depth][:, :, _win_perm()]),
        "w_out": f(np.asarray(inputs["w_out"])[:depth]),
        "w_up": f(np.asarray(inputs["w_up"])[:depth]),
        "w_down": f(np.asarray(inputs["w_down"])[:depth]),
        "ple_proj": f(np.asarray(inputs["ple_proj"])[:depth]),
        "ple_gate": f(np.asarray(inputs["ple_gate"])[:depth]),
        "oh": _onehots(),
        "ident": np.eye(128, dtype=np.float32),
        "maskF": (i[:, None] <= i[None, :]).astype(np.float32),
        "maskB": (i[:, None] >= i[None, :]).astype(np.float32),
        "sel": _selectors(),
    }
    return m


_CACHE = {}


def kernel(**inputs):
    x = np.asarray(inputs["x"])
    B, S, _ = x.shape
    depth = np.asarray(inputs["w_in"]).shape[0]
    key = (S, depth)
    if key not in _CACHE:
        _CACHE[key] = build(S, depth)[0]
    nc = _CACHE[key]
    shared = None
    in_maps = []
    for b in range(B):
        m = host_inputs(inputs, S, depth, b) if shared is None else dict(shared)
        if shared is None:
            shared = {k: v for k, v in m.items() if k not in ("xT", "pT")}
        else:
            m["xT"] = np.ascontiguousarray(x[b].T.astype(np.float32))
            m["pT"] = np.ascontiguousarray(np.transpose(np.asarray(inputs["p"])[:depth, b], (0, 2, 1)).astype(np.float32))
        in_maps.append(m)
    res = run_bass_kernel_spmd(nc, in_maps, core_ids=list(range(B)))
    out = np.stack([np.ascontiguousarray(r["outT"].T) for r in res.results], axis=0)
    return out.astype(np.float32)
```
